# Optimizing a Trainium2 kernel written in Bass

```python
import numpy as np
import jax
import jax.numpy as jnp
from jax import lax

D_MODEL = 2048
BATCH = 8
SEQ = 4096
DEPTH = 4
DEC_BATCH = 32
DEC_SEQ = 64
PAST_LEN = 1024

CHUNK = 64
MIX_W = D_MODEL
ATT_W = MIX_W // 2
SSM_W = MIX_W // 4
POOL_W = MIX_W - ATT_W - SSM_W
N_HEADS = 16
HEAD_DIM = ATT_W // N_HEADS
N_PREV_CHUNKS = 8
PREV_ROWS = N_PREV_CHUNKS * CHUNK
BAND_ROWS = PREV_ROWS + CHUNK
REL_CLIP = 256
SSM_GC = 16
SSM_GROUPS = SSM_W // SSM_GC
SSM_STATE = 64
DT_MIN = 1e-3
DT_MAX = 1e-1
POOL_WINDOWS = (2, 4, 8, 16)
POOL_GROUP_W = POOL_W // len(POOL_WINDOWS)
POOL_HIST = max(POOL_WINDOWS) - 1
IN_COLS = 4 * ATT_W + 2 * SSM_W + 2 * POOL_W
IN_SPLITS = (ATT_W, 2 * ATT_W, 3 * ATT_W, 4 * ATT_W, 4 * ATT_W + SSM_W,
             4 * ATT_W + 2 * SSM_W, 4 * ATT_W + 2 * SSM_W + POOL_W)
EPS = 1e-6
NEG_INF = -1e30

kernel_name = 'hymba_streaming_encoder_step'


def rmsnorm(x, g):
    xf = x.astype(jnp.float32)
    y = xf * lax.rsqrt(jnp.mean(xf * xf, axis=-1, keepdims=True) + EPS)
    return y.astype(x.dtype) * g


def chunk_band_attention(q, k, v, hist_k, hist_v, start_pos, rel_bias):
    B, L, H, Dh = q.shape
    W = hist_k.shape[1]
    n_chunks = -(-L // CHUNK)
    tail = n_chunks * CHUNK - L
    lead = PREV_ROWS - W
    end_pos = start_pos + L

    def extend(hist, new):
        return jnp.concatenate([jnp.zeros((B, lead, H, Dh), new.dtype), hist.astype(new.dtype), new,
                                jnp.zeros((B, tail, H, Dh), new.dtype)], axis=1)

    k_ext = extend(hist_k, k)
    v_ext = extend(hist_v, v)
    q_pad = jnp.pad(q, ((0, 0), (0, tail), (0, 0), (0, 0)))
    scale = HEAD_DIM ** -0.5

    def one_chunk(n):
        q0 = n * CHUNK
        qc = lax.dynamic_slice_in_dim(q_pad, q0, CHUNK, axis=1)
        kc = lax.dynamic_slice_in_dim(k_ext, q0, BAND_ROWS, axis=1)
        vc = lax.dynamic_slice_in_dim(v_ext, q0, BAND_ROWS, axis=1)
        q_pos = start_pos + q0 + jnp.arange(CHUNK)
        k_pos = start_pos - PREV_ROWS + q0 + jnp.arange(BAND_ROWS)
        q_ch = q_pos // CHUNK
        k_ch = k_pos // CHUNK
        mask = ((k_pos >= 0)[None, :] & (k_pos < end_pos)[None, :]
                & (k_ch[None, :] <= q_ch[:, None]) & (k_ch[None, :] >= q_ch[:, None] - N_PREV_CHUNKS))
        rel = jnp.clip(q_pos[:, None] - k_pos[None, :], -REL_CLIP, REL_CLIP) + REL_CLIP
        bias = rel_bias[:, rel].astype(jnp.float32)
        s = jnp.einsum('bqhd,bkhd->bhqk', qc, kc, preferred_element_type=jnp.float32) * scale + bias
        s = jnp.where(mask[None, None], s, NEG_INF)
        p = jax.nn.softmax(s, axis=-1).astype(vc.dtype)
        return jnp.einsum('bhqk,bkhd->bqhd', p, vc)

    out = lax.map(one_chunk, jnp.arange(n_chunks))
    out = jnp.moveaxis(out, 0, 1).reshape(B, n_chunks * CHUNK, H, Dh)
    return out[:, :L]


def _complex_affine_combine(e1, e2):
    a1r, a1i, b1r, b1i = e1
    a2r, a2i, b2r, b2i = e2
    ar = a2r * a1r - a2i * a1i
    ai = a2r * a1i + a2i * a1r
    br = a2r * b1r - a2i * b1i + b2r
    bi = a2r * b1i + a2i * b1r + b2i
    return ar, ai, br, bi


def s5_branch(u, h0_re, h0_im, a_re, a_im, log_dt, b_re, b_im, c_re, c_im, d_skip, w_glu, b_glu):
    f32 = jnp.float32
    Bsz, L, _ = u.shape
    uf = u.astype(f32).reshape(Bsz, L, SSM_GROUPS, SSM_GC)
    dt = jnp.exp(log_dt.astype(f32))[:, None]
    ar, ai = a_re.astype(f32), a_im.astype(f32)
    mag = jnp.exp(ar * dt)
    ang = ai * dt
    abar_r, abar_i = mag * jnp.cos(ang), mag * jnp.sin(ang)
    den = ar * ar + ai * ai
    nr, ni = abar_r - 1.0, abar_i
    coef_r = (nr * ar + ni * ai) / den
    coef_i = (ni * ar - nr * ai) / den
    br, bi = b_re.astype(f32), b_im.astype(f32)
    bbar_r = coef_r[..., None] * br - coef_i[..., None] * bi
    bbar_i = coef_r[..., None] * bi + coef_i[..., None] * br
    bu_r = jnp.einsum('blgc,gpc->blgp', uf, bbar_r)
    bu_i = jnp.einsum('blgc,gpc->blgp', uf, bbar_i)
    h0r, h0i = h0_re.astype(f32), h0_im.astype(f32)
    bu_r = bu_r.at[:, 0].add(abar_r * h0r - abar_i * h0i)
    bu_i = bu_i.at[:, 0].add(abar_r * h0i + abar_i * h0r)
    a_seq_r = jnp.broadcast_to(abar_r, (1, L, SSM_GROUPS, SSM_STATE))
    a_seq_i = jnp.broadcast_to(abar_i, (1, L, SSM_GROUPS, SSM_STATE))
    _, _, hr, hi = lax.associative_scan(_complex_affine_combine, (a_seq_r, a_seq_i, bu_r, bu_i), axis=1)
    y = (jnp.einsum('blgp,gcp->blgc', hr, c_re.astype(f32))
         - jnp.einsum('blgp,gcp->blgc', hi, c_im.astype(f32))
         + d_skip.astype(f32) * uf).reshape(Bsz, L, SSM_W)
    g = jax.nn.gelu(y)
    gl = g @ w_glu.astype(f32) + b_glu.astype(f32)
    out = gl[..., :SSM_W] * jax.nn.sigmoid(gl[..., SSM_W:])
    return out.astype(u.dtype), hr[:, -1], hi[:, -1]


def pool_mix(ext, first_pos, w_pool, pool_scale):
    f32 = jnp.float32
    B, E, _ = ext.shape
    L = E - POOL_HIST
    ef = ext.astype(f32)
    cs = jnp.concatenate([jnp.zeros((B, 1, POOL_W), f32), jnp.cumsum(ef, axis=1)], axis=1)
    pos = first_pos + POOL_HIST + jnp.arange(L)
    tok = ef[:, POOL_HIST:]
    outs = []
    for g, w in enumerate(POOL_WINDOWS):
        lo, hi = g * POOL_GROUP_W, (g + 1) * POOL_GROUP_W
        wsum = cs[:, POOL_HIST + 1:POOL_HIST + 1 + L, lo:hi] - cs[:, POOL_HIST + 1 - w:POOL_HIST + 1 - w + L, lo:hi]
        cnt = jnp.minimum(w, pos + 1).astype(f32)
        diff = wsum / cnt[None, :, None] - tok[..., lo:hi]
        outs.append(jnp.einsum('blc,cd->bld', diff, w_pool[g].astype(f32)))
    return (jnp.concatenate(outs, axis=-1) * pool_scale.astype(f32)).astype(ext.dtype)


def trunk_layer(x, c, hist_k, hist_v, h0_re, h0_im, pool_hist, start_pos,
                norm_g, w_ada, b_ada, w_in, rel_bias, a_re, a_im, log_dt, b_re, b_im, c_re, c_im,
                d_skip, w_glu, b_glu, w_pool, pool_scale, branch_g, w_out):
    B, L, _ = x.shape
    mod = jax.nn.silu(c) @ w_ada + b_ada
    shift, scale, gate = jnp.split(mod, 3, axis=-1)
    h = rmsnorm(x, norm_g) * (1.0 + scale[:, None]) + shift[:, None]
    proj = h @ w_in
    q, k, v, z_att, u_ssm, z_ssm, u_pool, z_pool = jnp.split(proj, IN_SPLITS, axis=-1)
    hd = (B, L, N_HEADS, HEAD_DIM)
    q, k, v = q.reshape(hd), k.reshape(hd), v.reshape(hd)
    y_att = chunk_band_attention(q, k, v, hist_k, hist_v, start_pos, rel_bias).reshape(B, L, ATT_W)
    y_ssm, hT_re, hT_im = s5_branch(u_ssm, h0_re, h0_im, a_re, a_im, log_dt, b_re, b_im,
                                    c_re, c_im, d_skip, w_glu, b_glu)
    pool_ext = jnp.concatenate([pool_hist.astype(u_pool.dtype), u_pool], axis=1)
    y_pool = pool_mix(pool_ext, start_pos - POOL_HIST, w_pool, pool_scale)
    g_att, g_ssm, g_pool = jnp.split(branch_g, (ATT_W, ATT_W + SSM_W))
    y = jnp.concatenate([rmsnorm(y_att, g_att) * jax.nn.silu(z_att),
                         rmsnorm(y_ssm, g_ssm) * jax.nn.silu(z_ssm),
                         rmsnorm(y_pool, g_pool) * jax.nn.silu(z_pool)], axis=-1)
    x = x + gate[:, None] * (y @ w_out)
    return x, k, v, hT_re, hT_im, pool_ext[:, -POOL_HIST:]


def setup_inputs(seed: int = 0) -> dict:
    key = jax.random.key(seed)
    ks = iter(jax.random.split(key, 40))
    f32 = jnp.float32

    def nrm(shape, s):
        return s * jax.random.normal(next(ks), shape, f32)

    att_hist = min(PREV_ROWS, PAST_LEN)
    n_idx = jnp.arange(SSM_STATE, dtype=f32)
    return {
        'x_prompt': nrm((BATCH, SEQ, D_MODEL), 1.0),
        'x_sample': nrm((DEC_BATCH, DEC_SEQ, D_MODEL), 1.0),
        'c_prompt': nrm((BATCH, D_MODEL), 1.0),
        'c_sample': nrm((DEC_BATCH, D_MODEL), 1.0),
        'cache_k': nrm((DEPTH, DEC_BATCH, att_hist, N_HEADS, HEAD_DIM), 1.0),
        'cache_v': nrm((DEPTH, DEC_BATCH, att_hist, N_HEADS, HEAD_DIM), 1.0),
        'state_ssm_re': nrm((DEPTH, DEC_BATCH, SSM_GROUPS, SSM_STATE), 0.1),
        'state_ssm_im': nrm((DEPTH, DEC_BATCH, SSM_GROUPS, SSM_STATE), 0.1),
        'state_pool': nrm((DEPTH, DEC_BATCH, POOL_HIST, POOL_W), 1.0),
        'norm_g': 1.0 + nrm((DEPTH, D_MODEL), 0.05),
        'w_ada': nrm((DEPTH, D_MODEL, 3 * D_MODEL), 0.5 * D_MODEL ** -0.5),
        'b_ada': nrm((DEPTH, 3 * D_MODEL), 0.02),
        'w_in': nrm((DEPTH, D_MODEL, IN_COLS), D_MODEL ** -0.5),
        'rel_bias': nrm((DEPTH, N_HEADS, 2 * REL_CLIP + 1), 0.5),
        'ssm_a_re': -0.5 + nrm((DEPTH, SSM_GROUPS, SSM_STATE), 0.01),
        'ssm_a_im': jnp.pi * n_idx + nrm((DEPTH, SSM_GROUPS, SSM_STATE), 0.01),
        'ssm_log_dt': jax.random.uniform(next(ks), (DEPTH, SSM_GROUPS), f32,
                                         minval=float(np.log(DT_MIN)), maxval=float(np.log(DT_MAX))),
        'ssm_b_re': nrm((DEPTH, SSM_GROUPS, SSM_STATE, SSM_GC), SSM_GC ** -0.5),
        'ssm_b_im': nrm((DEPTH, SSM_GROUPS, SSM_STATE, SSM_GC), SSM_GC ** -0.5),
        'ssm_c_re': nrm((DEPTH, SSM_GROUPS, SSM_GC, SSM_STATE), SSM_STATE ** -0.5),
        'ssm_c_im': nrm((DEPTH, SSM_GROUPS, SSM_GC, SSM_STATE), SSM_STATE ** -0.5),
        'ssm_d': nrm((DEPTH, SSM_GROUPS, SSM_GC), 0.5),
        'w_glu': nrm((DEPTH, SSM_W, 2 * SSM_W), SSM_W ** -0.5),
        'b_glu': nrm((DEPTH, 2 * SSM_W), 0.02),
        'w_pool': nrm((DEPTH, len(POOL_WINDOWS), POOL_GROUP_W, POOL_GROUP_W), POOL_GROUP_W ** -0.5),
        'pool_scale': 1.0 + nrm((DEPTH, POOL_W), 0.1),
        'branch_norm_g': 1.0 + nrm((DEPTH, MIX_W), 0.05),
        'w_out': nrm((DEPTH, MIX_W, D_MODEL), MIX_W ** -0.5),
        'final_norm_g': 1.0 + nrm((D_MODEL,), 0.05),
    }


def reference(x_prompt, x_sample, c_prompt, c_sample, cache_k, cache_v, state_ssm_re, state_ssm_im,
              state_pool, norm_g, w_ada, b_ada, w_in, rel_bias, ssm_a_re, ssm_a_im, ssm_log_dt,
              ssm_b_re, ssm_b_im, ssm_c_re, ssm_c_im, ssm_d, w_glu, b_glu, w_pool, pool_scale,
              branch_norm_g, w_out, final_norm_g):
    Bp, Lp, _ = x_prompt.shape
    keep = min(PREV_ROWS, Lp)
    hk0 = jnp.zeros((Bp, 0, N_HEADS, HEAD_DIM), x_prompt.dtype)
    h00 = jnp.zeros((Bp, SSM_GROUPS, SSM_STATE), jnp.float32)
    ph0 = jnp.zeros((Bp, POOL_HIST, POOL_W), x_prompt.dtype)
    xp, xs = x_prompt, x_sample
    kp_l, vp_l, srp_l, sip_l, pp_l = [], [], [], [], []
    ks_l, vs_l, srs_l, sis_l, ps_l = [], [], [], [], []
    for l in range(DEPTH):
        lw = (norm_g[l], w_ada[l], b_ada[l], w_in[l], rel_bias[l], ssm_a_re[l], ssm_a_im[l],
              ssm_log_dt[l], ssm_b_re[l], ssm_b_im[l], ssm_c_re[l], ssm_c_im[l], ssm_d[l],
              w_glu[l], b_glu[l], w_pool[l], pool_scale[l], branch_norm_g[l], w_out[l])
        xp, k_p, v_p, sr_p, si_p, pool_p = trunk_layer(xp, c_prompt, hk0, hk0, h00, h00, ph0, 0, *lw)
        kp_l.append(k_p[:, Lp - keep:])
        vp_l.append(v_p[:, Lp - keep:])
        srp_l.append(sr_p)
        sip_l.append(si_p)
        pp_l.append(pool_p)
        xs, k_s, v_s, sr_s, si_s, pool_s = trunk_layer(xs, c_sample, cache_k[l], cache_v[l], state_ssm_re[l],
                                                       state_ssm_im[l], state_pool[l], PAST_LEN, *lw)
        ks_l.append(k_s)
        vs_l.append(v_s)
        srs_l.append(sr_s)
        sis_l.append(si_s)
        ps_l.append(pool_s)
    y_prompt = rmsnorm(xp, final_norm_g)
    y_sample = rmsnorm(xs, final_norm_g)
    return (y_prompt, y_sample,
            jnp.stack(kp_l), jnp.stack(vp_l), jnp.stack(srp_l), jnp.stack(sip_l), jnp.stack(pp_l),
            jnp.stack(ks_l), jnp.stack(vs_l), jnp.stack(srs_l), jnp.stack(sis_l), jnp.stack(ps_l))
```

```python
import contextlib
import numpy as np
import concourse.bass as bass
import concourse.mybir as mybir
from concourse.bass_utils import run_bass_kernel_spmd

F32 = mybir.dt.float32
BF16 = mybir.dt.bfloat16
AF = mybir.ActivationFunctionType
ALU = mybir.AluOpType

NL = 4
D = 2048
KT = 16
TT = 512
NPT_FULL = 8
EPS = 1e-6
PI = float(np.pi)
ENGS = ('pe', 'act', 'dve', 'pool', 'sp')
DMA_KEYS = []


class Prog:
    def __init__(self, nc, es):
        self.nc = nc
        self.es = es
        self.ops = {e: [] for e in ENGS}
        self.cnt = {e: 0 for e in ENGS}
        self.sem = {e: es.enter_context(nc.semaphore("s_" + e)) for e in ('pe', 'act', 'dve', 'pool')}
        self.dsem = {}
        self.dcum = {}
        self.last_w = {}
        self.readers = {}
        self.out_toks = []

    def dma_sem(self, key):
        if key not in self.dsem:
            self.dsem[key] = [self.es.enter_context(self.nc.semaphore("d_%d" % len(self.dsem))), 0]
            self.dcum[id(self.dsem[key][0])] = self.dsem[key]
            DMA_KEYS.append(key)
        return self.dsem[key]

    def op(self, eng, fn, reads=(), writes=(), dkey=None, out=False):
        deps = {}

        def add(tok):
            if tok is None:
                return
            s, v = tok
            cum = self.dcum.get(id(s))
            if cum is not None:
                v = cum[1]
            if deps.get(id(s), (None, 0))[1] < v:
                deps[id(s)] = (s, v)

        for k in reads:
            add(self.last_w.get(k))
        for k in writes:
            add(self.last_w.get(k))
            for t in self.readers.get(k, {}).values():
                add(t)
        if dkey is None:
            self.cnt[eng] += 1
            tok = (self.sem[eng], self.cnt[eng])
            inc = (self.sem[eng], 1)
        else:
            d = self.dma_sem(dkey)
            d[1] += 16
            tok = (d[0], d[1])
            inc = (d[0], 16)
        self.ops[eng].append((list(deps.values()), fn, inc))
        for k in reads:
            r = self.readers.setdefault(k, {})
            if r.get(id(tok[0]), (None, 0))[1] < tok[1]:
                r[id(tok[0])] = tok
        for k in writes:
            self.last_w[k] = tok
            self.readers[k] = {}
        if out:
            self.out_toks.append(tok)
        return tok

    def barrier(self):
        toks = [(self.sem[e], self.cnt[e]) for e in ('pe', 'act', 'dve', 'pool') if self.cnt[e] > 0]
        toks += [(d[0], d[1]) for d in self.dsem.values() if d[1] > 0]
        for e in ENGS:
            self.ops[e].append((list(toks), None, None))

    def finish(self):
        toks = {}
        for s, v in self.out_toks:
            if toks.get(id(s), (None, 0))[1] < v:
                toks[id(s)] = (s, v)
        self.ops['sp'].append((list(toks.values()), None, None))

    def emit(self):
        nc = self.nc
        sem_pe = self.sem['pe']
        ops = self.ops

        def run(eng, e):
            seen = {}
            for deps, fn, inc in ops[eng]:
                for s, v in deps:
                    if eng == 'pe' and s is sem_pe:
                        continue
                    if seen.get(id(s), 0) >= v:
                        continue
                    seen[id(s)] = v
                    e.wait_ge(s, v)
                if fn is not None:
                    ins = fn(e)
                    ins.then_inc(inc[0], inc[1])

        with nc.Block() as block:
            @block.tensor
            def _(e):
                run('pe', e)

            @block.scalar
            def _(e):
                run('act', e)

            @block.vector
            def _(e):
                run('dve', e)

            @block.gpsimd
            def _(e):
                run('pool', e)

            @block.sync
            def _(e):
                run('sp', e)


DEBUG = [False]


class _Stop(Exception):
    pass


def _chk(name):
    if STOP[0] == name:
        raise _Stop()

STOP = [None]


def build_nc(NPT=NPT_FULL, with_sample=True, nlayers=NL):
    nc = bass.Bass("TRN2", target_bir_lowering=False)
    es = contextlib.ExitStack()
    with es:
        _build(nc, es, NPT, with_sample, nlayers)
    return nc


def _build(nc, es, NPT, with_sample, NLR):
    P = Prog(nc, es)
    NTP = NPT * TT

    def din(name, shape, dt=F32):
        return nc.dram_tensor(name, list(shape), dt, kind="ExternalInput").ap()

    def dout(name, shape):
        return nc.dram_tensor(name, list(shape), F32, kind="ExternalOutput").ap()

    def dscr(name, shape, dt):
        return nc.dram_tensor(name, list(shape), dt).ap()

    def sb(name, shape, dt=F32):
        return es.enter_context(nc.sbuf_tensor(name, list(shape), dt))

    def ps(name, shape, dt=F32):
        return es.enter_context(nc.psum_tensor(name, list(shape), dt))

    xp = din("xp", [NTP, D]); xs = din("xs", [256, D]); cvec = din("cvec", [5, D])
    ck = din("ck", [NL, 4, 512, 1024]); cv = din("cv", [NL, 4, 512, 1024])
    sre = din("sre", [NL, 4, 32, 64]); sim = din("sim", [NL, 4, 32, 64]); spool = din("spool", [NL, 4, 15, 512])
    norm_g = din("norm_g", [NL, D]); w_ada = din("w_ada", [NL, D, 3 * D]); b_ada = din("b_ada", [NL, 3 * D])
    w_in = din("w_in", [NL, D, 6144]); xrb = din("xrb", [NL, 16, 832])
    a_re = din("a_re", [NL, 32, 64]); a_im = din("a_im", [NL, 32, 64]); log_dt = din("log_dt", [NL, 32])
    b_re = din("b_re", [NL, 32, 64, 16]); b_im = din("b_im", [NL, 32, 64, 16])
    c_re = din("c_re", [NL, 32, 16, 64]); c_im = din("c_im", [NL, 32, 16, 64])
    ssm_d = din("ssm_d", [NL, 512]); w_glu = din("w_glu", [NL, 512, 1024]); b_glu = din("b_glu", [NL, 1024])
    w_pool = din("w_pool", [NL, 4, 128, 128]); pool_scale = din("pool_scale", [NL, 512])
    bng = din("bng", [NL, D]); w_out = din("w_out", [NL, D, D]); fng = din("fng", [D])
    identf_d = din("identf", [128, 128]); j64_d = din("j64", [64, 64]); invcnt_d = din("invcnt", [128, 4, 16])
    masks_d = din("masks", [128, 2]); bmask_d = din("bmask", [128, 128])

    yp = dout("yp", [NTP, D]); ys = dout("ys", [256, D])
    nkp = dout("nkp", [NL, 512, 1024]); nvp = dout("nvp", [NL, 512, 1024])
    nsrp = dout("nsrp", [NL, 32, 64]); nsip = dout("nsip", [NL, 32, 64]); npp = dout("npp", [NL, 15, 512])
    nks = dout("nks", [NL, 256, 1024]); nvs = dout("nvs", [NL, 256, 1024])
    nsrs = dout("nsrs", [NL, 4, 32, 64]); nsis = dout("nsis", [NL, 4, 32, 64]); nps = dout("nps", [NL, 4, 15, 512])

    eb_d = dscr("eb_d", [NL, 2, 128, 8 * 640], BF16)
    lam_d = dscr("lam_d", [NL, 128, 16 * 9 * 3], F32)
    wz_d = dscr("wz_d", [NL, 16, 128, 4, 2, 128], BF16)
    bd_d = dscr("bd_d", [NL, 16, 128, 4, 128], BF16)
    wy_d = dscr("wy_d", [NL, 16, 128, 16, 2, 32], BF16)
    kh_d = dscr("kh_d", [NL, 128, 8, 512], BF16)
    vh_d = dscr("vh_d", [NL, 128, 4, 1024], BF16)

    xT = sb("xT", [128, KT, TT])
    hy = sb("hy", [128, KT, TT], BF16)
    NWB = 3
    wblk = [sb("wblk%d" % i, [128, KT, 128], BF16) for i in range(NWB)]
    qT = sb("qT", [128, 8, TT], BF16)
    kT = sb("kT", [128, 8, 1024], BF16)
    vb = sb("vb", [128, 8, 1024], BF16)
    zT = sb("zT", [128, KT, TT], BF16)
    ub = sb("ub", [128, 4, TT], BF16)
    UPW = 528
    upT = sb("upT", [128, 4, UPW])
    pta = sb("pta", [128, UPW]); ptb = sb("ptb", [128, UPW])
    ebt = sb("ebt", [128, 8, 640], BF16)
    lam = sb("lam", [128, 16, 9, 3])
    wzb = sb("wzb", [128, 4096], BF16)
    bdb = sb("bdb", [128, 16, 128], BF16)
    Zr = sb("Zr", [128, 16, 32]); Zi = sb("Zi", [128, 16, 32])
    Sp = sb("Sp", [128, 2, 16, 32], BF16)
    bmask = sb("bmask_s", [128, 128])
    stR = sb("stR", [128, NL, 16, 4]); stI = sb("stI", [128, NL, 16, 4]); uph = sb("uph", [128, NL, 4, 15])
    gT = sb("gT", [128, 4, TT], BF16)
    wpl = sb("wpl", [128, 4, 128], BF16)
    identf = sb("identf_s", [128, 128]); identb = sb("identb", [128, 128], BF16)
    onesb = sb("onesb", [128, 128], BF16); j64 = sb("j64_s", [64, 64])
    masks = sb("masks_s", [128, 2]); invcnt = sb("invcnt_s", [128, 4, 16])
    modT = sb("modT", [128, NL, 48, 5])
    Gm = sb("Gm", [128, NL, KT, 5])
    ngT = sb("ngT", [128, NL, KT]); bngT = sb("bngT", [128, NL, KT]); fngT = sb("fngT", [128, KT])
    badaT = sb("badaT", [128, NL, 48]); pscT = sb("pscT", [128, NL, 4]); bgluT = sb("bgluT", [128, NL, 8])
    dskT = sb("dskT", [128, NL, 4])
    scT = sb("scT", [128, KT, 8], BF16)
    rstd = sb("rstd", [128, TT]); tmpf = [sb("tmpf%d" % i, [128, TT]) for i in range(2)]
    sqb = [sb("sqb%d" % i, [128, TT], BF16) for i in range(2)]
    ef = [sb("ef%d" % i, [128, 640]) for i in range(2)]
    pt16 = [sb("pt16_%d" % i, [128, 640], BF16) for i in range(2)]
    rden = [sb("rden%d" % i, [128, 128]) for i in range(2)]
    stg = sb("stg", [128, D])
    kvs = [sb("kvs%d" % i, [128, 128]) for i in range(4)]
    ckb = [sb("ckb%d" % i, [128, 1024], BF16) for i in range(1)]

    psA = ps("psA", [128, 1024])
    psM = [ps("psM%d" % i, [128, 512]) for i in range(3)]
    psO = ps("psO", [128, 512])
    psT = ps("psT", [128, 512])
    psTb = ps("psTb", [128, 1024], BF16)

    mmi = [0]

    def next_psM():
        mmi[0] = (mmi[0] + 1) % 3
        return mmi[0]

    cnt = {}

    def rot(name, n):
        cnt[name] = (cnt.get(name, -1) + 1) % n
        return cnt[name]

    def dma(eng, out_ap, in_ap, key, reads=(), writes=(), out=False, nonc=False):
        def fn(e):
            if nonc:
                with nc.allow_non_contiguous_dma(reason="small strided parameter/state transfers"):
                    return e.dma_start(out=out_ap, in_=in_ap)
            return e.dma_start(out=out_ap, in_=in_ap)
        return P.op(eng, fn, reads=reads, writes=writes, dkey=key, out=out)

    def mm_group(out_ap, pairs, reads, writes, **kw):
        n = len(pairs)

        def fn(e):
            ins = None
            for i, (l_, r_) in enumerate(pairs):
                ins = e.matmul(out_ap, l_, r_, start=(i == 0), stop=(i == n - 1), **kw)
            return ins
        return P.op('pe', fn, reads=reads, writes=writes)

    dma('sp', identf[:], identf_d, 'c_ident', writes=['identf'])
    dma('sp', j64[:], j64_d, 'c_j64', writes=['j64'])
    dma('sp', masks[:], masks_d, 'c_masks', writes=['masks'])
    dma('sp', bmask[:], bmask_d, 'c_bmask', writes=['bmask'])
    dma('sp', invcnt[:], invcnt_d, 'c_invcnt', writes=['invcnt'])
    dma('pool', identb[:], identf_d, 'c_identb', writes=['identb'])
    P.op('dve', lambda e: e.memset(onesb[:], 1.0), writes=['onesb'])
    P.op('dve', lambda e: e.memset(kT[:], 0.0), writes=['kTh', 'kTn'])
    P.op('dve', lambda e: e.memset(vb[:], 0.0), writes=['vbh', 'vbn'])
    P.op('dve', lambda e: e.memset(upT[:], 0.0), writes=['upT'])
    P.op('dve', lambda e: e.memset(pta[:], 0.0), writes=['pta'])
    P.op('dve', lambda e: e.memset(ptb[:], 0.0), writes=['ptb'])
    P.op('dve', lambda e: e.memset(stR[:], 0.0), writes=['stR'])
    P.op('dve', lambda e: e.memset(stI[:], 0.0), writes=['stI'])
    P.op('dve', lambda e: e.memset(uph[:], 0.0), writes=['uph'])

    def fm_load(dst, src_row, key):
        dma('sp', dst, src_row.rearrange("(t p) -> p t", p=128), key, writes=[key], nonc=True)

    for l in range(NL):
        fm_load(ngT[:, l, :], norm_g[l], 'ngT%d' % l)
        fm_load(bngT[:, l, :], bng[l], 'bngT%d' % l)
        fm_load(badaT[:, l, :], b_ada[l], 'badaT%d' % l)
        fm_load(pscT[:, l, :], pool_scale[l], 'pscT%d' % l)
        fm_load(bgluT[:, l, :], b_glu[l], 'bgluT%d' % l)
        fm_load(dskT[:, l, :], ssm_d[l], 'dskT%d' % l)
    fm_load(fngT[:, :], fng, 'fngT')

    if STOP[0] == 's1':
        P.barrier(); P.finish(); P.emit(); return
    dma('sp', stg[0:5, :], cvec, 'stg_ld', writes=['stg'])
    P.op('act', lambda e: e.activation(out=stg[0:5, :], in_=stg[0:5, :], func=AF.Silu), reads=['stg'], writes=['stg'])

    def fn_ct(e):
        ins = None
        for kt in range(KT):
            ins = e.transpose(psT[:, kt * 8:kt * 8 + 5], stg[0:5, kt * 128:(kt + 1) * 128], identf[0:5, 0:5])
        return ins
    P.op('pe', fn_ct, reads=['stg', 'identf'], writes=['psT'])
    P.op('dve', lambda e: e.tensor_copy(out=scT[:, :, 0:5],
                                        in_=psT[:, 0:128].rearrange("p (k r) -> p k r", r=8)[:, :, 0:5]),
         reads=['psT'], writes=['scT'])

    wq = []
    wstate = {'issued': 0, 'used': 0}

    def w_issue_upto(n):
        while wstate['issued'] < min(n, len(wq)):
            i = wstate['issued']
            b = i % NWB
            kind_, src = wq[i]
            if kind_ == 'glu':
                wsrc, c4_ = src
                dma('pool', wblk[b][:, 0:4, :], wsrc[:, c4_ * 128:(c4_ + 1) * 128].rearrange("(k p) c -> p k c", p=128),
                    'wblk%d' % b, writes=['wblk%d' % b])
                dma('pool', wblk[b][:, 4:8, :],
                    wsrc[:, 512 + c4_ * 128:512 + (c4_ + 1) * 128].rearrange("(k p) c -> p k c", p=128),
                    'wblk%d' % b, writes=['wblk%d' % b])
            else:
                dma('pool', wblk[b][:, :, :], src.rearrange("(k p) c -> p k c", p=128), 'wblk%d' % b,
                    writes=['wblk%d' % b])
            wstate['issued'] += 1

    def w_next():
        i = wstate['used']
        w_issue_upto(i + NWB)
        wstate['used'] += 1
        return i % NWB

    IN_ORDER = list(range(8, 16)) + list(range(16, 24)) + list(range(32, 36)) + list(range(40, 44)) + \
        list(range(24, 32)) + list(range(36, 40)) + list(range(44, 48)) + list(range(0, 8))
    tiles = [('p', t) for t in range(NPT)] + ([('s', 0)] if with_sample else [])
    for l in range(NL):
        for cb in range(48):
            wq.append(('w', w_ada[l][:, cb * 128:(cb + 1) * 128]))
    for _tile in tiles:
        for l in range(NLR):
            for cb in IN_ORDER:
                wq.append(('w', w_in[l][:, cb * 128:(cb + 1) * 128]))
            for c4_ in range(4):
                wq.append(('glu', (w_glu[l], c4_)))
            for cb in range(16):
                wq.append(('w', w_out[l][:, cb * 128:(cb + 1) * 128]))

    for l in range(NL):
        for cb in range(48):
            b = w_next()
            m = next_psM()
            mm_group(psM[m][:, 0:5], [(wblk[b][:, kt, :], scT[:, kt, 0:5]) for kt in range(KT)],
                     reads=['wblk%d' % b, 'scT'], writes=['psM%d' % m])
            P.op('dve', (lambda l, cb, m: lambda e: e.tensor_scalar(
                out=modT[:, l, cb, :], in0=psM[m][:, 0:5], scalar1=badaT[:, l, cb:cb + 1], scalar2=None,
                op0=ALU.add))(l, cb, m),
                reads=['psM%d' % m, 'badaT%d' % l], writes=['modT'])
    for l in range(NL):
        P.op('dve', (lambda l: lambda e: e.tensor_scalar(out=Gm[:, l, :, :], in0=modT[:, l, 16:32, :], scalar1=1.0,
                                                         scalar2=None, op0=ALU.add))(l),
             reads=['modT'], writes=['Gm'])
        P.op('dve', (lambda l: lambda e: e.tensor_tensor(
            out=Gm[:, l, :, :], in0=Gm[:, l, :, :],
            in1=ngT[:, l, :].unsqueeze(2).broadcast_to([128, KT, 5]), op=ALU.mult))(l),
            reads=['Gm', 'ngT%d' % l], writes=['Gm'])

    if STOP[0] == 's2':
        P.barrier(); P.finish(); P.emit(); return
    tpair = stg[0:64, 0:2 * 704].rearrange("p (h r) -> p h r", h=2)
    for l in range(NLR):
        for par in range(2):
            for j in range(8):
                P.op('dve', lambda e: e.memset(stg[0:64, 0:1408], 0.0), writes=['stg'])
                src = bass.AP(xrb.tensor, (l * 16 + 2 * j) * 832, [[1, 64], [832, 2], [1, 576]])
                dma('sp', tpair[:, :, 64:640], src, 'stg_ld', writes=['stg'])
                P.op('act', lambda e: e.activation(out=tpair[:, :, 64:640], in_=tpair[:, :, 64:640], func=AF.Exp),
                     reads=['stg'], writes=['stg'])

                def fn(e, par=par):
                    ins = None
                    for p in range(5):
                        st = (64 + 128 * p) if par == 0 else 128 * p
                        for hh in range(2):
                            ins = e.matmul(psA[:, p * 128 + hh * 64:p * 128 + hh * 64 + 64],
                                           tpair[:, hh, st:st + 128], j64[:, :], start=True, stop=True)
                    return ins
                P.op('pe', fn, reads=['stg', 'j64'], writes=['psA'])
                P.op('dve', (lambda j: lambda e: e.tensor_copy(out=ebt[:, j, :], in_=psA[:, 0:640]))(j),
                     reads=['psA'], writes=['ebt'])
            dma('sp', eb_d[l, par], ebt[:].rearrange("p a c -> p (a c)"), 'eb_st', reads=['ebt'],
                writes=['eb_d%d_%d' % (l, par)])

    if STOP[0] == 's3':
        P.barrier(); P.finish(); P.emit(); return
    o = [0]

    def carve(n):
        a = stg[:, o[0]:o[0] + n]
        o[0] += n
        return a
    ar = carve(16); ai = carve(16); ldt = carve(16); dtv = carve(16); mag = carve(16); ang = carve(16)
    kacc = carve(16); rr = carve(16); sn = carve(16); cs_ = carve(16); t1 = carve(16); t2 = carve(16)
    cr_ = carve(16); ci_ = carve(16); den_ = carve(16)
    Br = carve(256); Bi = carve(256); Bbr = carve(256); Bbi = carve(256)
    Tm = carve(512)
    tB = Tm[:, 0:256]
    C2 = carve(128)
    Cr = Br; Ci = Bi
    assert o[0] <= 2048
    KS = 'stg'

    def V(fn_, extra_r=()):
        P.op('dve', fn_, reads=[KS] + list(extra_r), writes=[KS])

    def B3(a):
        return a.rearrange("p (a c) -> p a c", c=16)

    def bc16(a):
        return a.unsqueeze(2).broadcast_to([128, 16, 16])

    for l in range(NLR):
        for e_ in range(2):
            dma('sp', ar[64 * e_:64 * e_ + 64, :], bass.AP(a_re.tensor, (l * 32 + e_) * 64, [[1, 64], [128, 16]]),
                'stg_ld', writes=[KS], nonc=True)
            dma('sp', ai[64 * e_:64 * e_ + 64, :], bass.AP(a_im.tensor, (l * 32 + e_) * 64, [[1, 64], [128, 16]]),
                'stg_ld', writes=[KS], nonc=True)
            dma('sp', ldt[64 * e_:64 * e_ + 64, :], bass.AP(log_dt.tensor, l * 32 + e_, [[0, 64], [2, 16]]),
                'stg_ld', writes=[KS], nonc=True)
            dma('sp', B3(Br[64 * e_:64 * e_ + 64, :]),
                bass.AP(b_re.tensor, (l * 32 + e_) * 1024, [[16, 64], [2048, 16], [1, 16]]),
                'stg_ld', writes=[KS], nonc=True)
            dma('sp', B3(Bi[64 * e_:64 * e_ + 64, :]),
                bass.AP(b_im.tensor, (l * 32 + e_) * 1024, [[16, 64], [2048, 16], [1, 16]]),
                'stg_ld', writes=[KS], nonc=True)
        P.op('act', lambda e: e.activation(out=dtv, in_=ldt, func=AF.Exp), reads=[KS], writes=[KS])
        V(lambda e: e.tensor_tensor(out=mag, in0=ar, in1=dtv, op=ALU.mult))
        P.op('act', lambda e: e.activation(out=mag, in_=mag, func=AF.Exp), reads=[KS], writes=[KS])
        V(lambda e: e.tensor_tensor(out=ang, in0=ai, in1=dtv, op=ALU.mult))
        for shift, dst in ((0.0, sn), (PI / 2, cs_)):
            V((lambda shift: lambda e: e.tensor_scalar(out=rr, in0=ang, scalar1=shift, scalar2=None, op0=ALU.add))(shift))
            V(lambda e: e.memset(kacc, 0.0))
            for m_ in range(1, 9):
                V((lambda m_: lambda e: e.scalar_tensor_tensor(out=kacc, in0=rr, scalar=(2 * m_ - 1) * PI, in1=kacc,
                                                               op0=ALU.is_ge, op1=ALU.add))(m_))
            V(lambda e: e.scalar_tensor_tensor(out=rr, in0=kacc, scalar=-2 * PI, in1=rr, op0=ALU.mult, op1=ALU.add))
            P.op('act', (lambda dst: lambda e: e.activation(out=dst, in_=rr, func=AF.Sin))(dst),
                 reads=[KS], writes=[KS])
        V(lambda e: e.tensor_tensor(out=lam[:, :, 0, 0], in0=mag, in1=cs_, op=ALU.mult))
        V(lambda e: e.tensor_tensor(out=lam[:, :, 0, 1], in0=mag, in1=sn, op=ALU.mult))
        for k in range(1, 9):
            V((lambda k: lambda e: e.tensor_tensor(out=t1, in0=lam[:, :, k - 1, 0], in1=lam[:, :, k - 1, 0], op=ALU.mult))(k))
            V((lambda k: lambda e: e.tensor_tensor(out=t2, in0=lam[:, :, k - 1, 1], in1=lam[:, :, k - 1, 1], op=ALU.mult))(k))
            V((lambda k: lambda e: e.tensor_tensor(out=lam[:, :, k, 0], in0=t1, in1=t2, op=ALU.subtract))(k))
            V((lambda k: lambda e: e.tensor_tensor(out=t1, in0=lam[:, :, k - 1, 0], in1=lam[:, :, k - 1, 1], op=ALU.mult))(k))
            V((lambda k: lambda e: e.tensor_scalar(out=lam[:, :, k, 1], in0=t1, scalar1=2.0, scalar2=None, op0=ALU.mult))(k))
        V(lambda e: e.tensor_scalar(out=lam[:, :, :, 2], in0=lam[:, :, :, 1], scalar1=-1.0, scalar2=None, op0=ALU.mult))
        dma('sp', lam_d[l], lam[:].rearrange("p a b c -> p (a b c)"), 'lam_st', reads=[KS], writes=['lam_d%d' % l])
        V(lambda e: e.tensor_tensor(out=den_, in0=ar, in1=ar, op=ALU.mult))
        V(lambda e: e.tensor_tensor(out=t1, in0=ai, in1=ai, op=ALU.mult))
        V(lambda e: e.tensor_tensor(out=den_, in0=den_, in1=t1, op=ALU.add))
        V(lambda e: e.reciprocal(out=den_, in_=den_))
        V(lambda e: e.tensor_scalar(out=t1, in0=lam[:, :, 0, 0], scalar1=-1.0, scalar2=None, op0=ALU.add))
        V(lambda e: e.tensor_tensor(out=cr_, in0=t1, in1=ar, op=ALU.mult))
        V(lambda e: e.tensor_tensor(out=t2, in0=lam[:, :, 0, 1], in1=ai, op=ALU.mult))
        V(lambda e: e.tensor_tensor(out=cr_, in0=cr_, in1=t2, op=ALU.add))
        V(lambda e: e.tensor_tensor(out=cr_, in0=cr_, in1=den_, op=ALU.mult))
        V(lambda e: e.tensor_tensor(out=ci_, in0=lam[:, :, 0, 1], in1=ar, op=ALU.mult))
        V(lambda e: e.tensor_tensor(out=t2, in0=t1, in1=ai, op=ALU.mult))
        V(lambda e: e.tensor_tensor(out=ci_, in0=ci_, in1=t2, op=ALU.subtract))
        V(lambda e: e.tensor_tensor(out=ci_, in0=ci_, in1=den_, op=ALU.mult))
        V(lambda e: e.tensor_tensor(out=B3(Bbr), in0=B3(Br), in1=bc16(cr_), op=ALU.mult))
        V(lambda e: e.tensor_tensor(out=B3(tB), in0=B3(Bi), in1=bc16(ci_), op=ALU.mult))
        V(lambda e: e.tensor_tensor(out=Bbr, in0=Bbr, in1=tB, op=ALU.subtract))
        V(lambda e: e.tensor_tensor(out=B3(Bbi), in0=B3(Bi), in1=bc16(cr_), op=ALU.mult))
        V(lambda e: e.tensor_tensor(out=B3(tB), in0=B3(Br), in1=bc16(ci_), op=ALU.mult))
        V(lambda e: e.tensor_tensor(out=Bbi, in0=Bbi, in1=tB, op=ALU.add))
        for csrc, cdst in ((c_re, Cr), (c_im, Ci)):
            for blk in range(2):
                for p8 in range(8):
                    g0 = 2 * (8 * blk + p8)
                    dma('sp', C2[16 * p8:16 * p8 + 16, :].rearrange("c (e p) -> c e p", e=2),
                        bass.AP(csrc.tensor, (l * 32 + g0) * 1024, [[64, 16], [1024, 2], [1, 64]]),
                        'stg_ld', writes=[KS])
                P.op('pe', lambda e: e.transpose(psT[:, 0:128], C2, identf[:, :]), reads=[KS, 'identf'],
                     writes=['psT'])
                P.op('dve', (lambda cdst, blk: lambda e: e.tensor_copy(out=cdst[:, blk * 128:(blk + 1) * 128],
                                                                       in_=psT[:, 0:128]))(cdst, blk),
                     reads=['psT'], writes=[KS])
        arena = xT[:].rearrange("p k c -> p (k c)")
        ao = [0]

        def acarve(n):
            a_ = arena[:, ao[0]:ao[0] + n]
            ao[0] += n
            return a_
        zr = acarve(256); zi = acarve(256); TmR = acarve(512); TmI = acarve(512); CMr = acarve(512); CMi = acarve(512)
        cur_r = acarve(16); cur_i = acarve(16); nr_ = acarve(16); ni_ = acarve(16)
        yr = acarve(256); yi = acarve(256); ta = acarve(256)
        KX = 'xT'

        def VX(fn_, r=(), w=()):
            P.op('dve', fn_, reads=[KS, KX, 'masks'] + list(r), writes=[KX] + list(w))
        T4 = lambda a_: a_.rearrange("p (a e c) -> p a e c", e=2, c=16)
        for e_ in range(2):
            VX((lambda e_: lambda e: e.tensor_scalar(out=T4(CMr)[:, :, e_, :], in0=B3(Cr), scalar1=masks[:, e_:e_ + 1],
                                                     scalar2=None, op0=ALU.mult))(e_))
            VX((lambda e_: lambda e: e.tensor_scalar(out=T4(CMi)[:, :, e_, :], in0=B3(Ci), scalar1=masks[:, e_:e_ + 1],
                                                     scalar2=-1.0, op0=ALU.mult, op1=ALU.mult))(e_))
        VX(lambda e: e.memset(cur_r, 1.0))
        VX(lambda e: e.memset(cur_i, 0.0))
        wzs = wzb[:, 0:1024].rearrange("p (j r c) -> p j r c", r=2, c=128)
        wys = wzb[:, 1024:2048].rearrange("p (a r c) -> p a r c", r=2, c=32)
        for tau in range(17):
            if tau <= 15:
                VX(lambda e: e.tensor_tensor(out=B3(zr), in0=B3(Bbr), in1=bc16(cur_r), op=ALU.mult))
                VX(lambda e: e.tensor_tensor(out=B3(ta), in0=B3(Bbi), in1=bc16(cur_i), op=ALU.mult))
                VX(lambda e: e.tensor_tensor(out=zr, in0=zr, in1=ta, op=ALU.subtract))
                VX(lambda e: e.tensor_tensor(out=B3(zi), in0=B3(Bbi), in1=bc16(cur_r), op=ALU.mult))
                VX(lambda e: e.tensor_tensor(out=B3(ta), in0=B3(Bbr), in1=bc16(cur_i), op=ALU.mult))
                VX(lambda e: e.tensor_tensor(out=zi, in0=zi, in1=ta, op=ALU.add))
                for e_ in range(2):
                    VX((lambda e_: lambda e: e.tensor_scalar(out=T4(TmR)[:, :, e_, :], in0=B3(zr), scalar1=masks[:, e_:e_ + 1],
                                                             scalar2=None, op0=ALU.mult))(e_))
                    VX((lambda e_: lambda e: e.tensor_scalar(out=T4(TmI)[:, :, e_, :], in0=B3(zi), scalar1=masks[:, e_:e_ + 1],
                                                             scalar2=None, op0=ALU.mult))(e_))
                for ri, TmX in ((0, TmR), (1, TmI)):
                    def fn_t(e, TmX=TmX):
                        ins = None
                        for j in range(4):
                            ins = e.transpose(psT[:, j * 128:(j + 1) * 128], TmX[:, j * 128:(j + 1) * 128], identf[:, :])
                        return ins
                    P.op('pe', fn_t, reads=[KX, 'identf'], writes=['psT'])
                    P.op('dve', (lambda ri: lambda e: e.tensor_copy(out=wzs[:, :, ri, :],
                                                                    in_=psT[:, :].rearrange("p (j c) -> p j c", c=128)))(ri),
                         reads=['psT'], writes=['wzb'])
                dma('sp', wz_d[l, 15 - tau].rearrange("p j r c -> p (j r c)"), wzb[:, 0:1024], 'wz_st', reads=['wzb'],
                    writes=['wz_d%d' % l])
                for j in range(4):
                    m = next_psM()

                    def fn_bd(e, j=j, m=m):
                        e.matmul(psM[m][:, 0:128], TmR[:, j * 128:(j + 1) * 128], CMr[:, j * 128:(j + 1) * 128],
                                 start=True, stop=False)
                        return e.matmul(psM[m][:, 0:128], TmI[:, j * 128:(j + 1) * 128], CMi[:, j * 128:(j + 1) * 128],
                                        start=False, stop=True)
                    P.op('pe', fn_bd, reads=[KX], writes=['psM%d' % m])
                    P.op('dve', (lambda j, m: lambda e: e.tensor_tensor(out=bdb[:, j, :], in0=psM[m][:, 0:128],
                                                                        in1=bmask[:, :], op=ALU.mult))(j, m),
                         reads=['psM%d' % m, 'bmask'], writes=['bdb'])
                dma('sp', bd_d[l, tau].rearrange("p j c -> p (j c)"), bdb[:, 0:4, :].rearrange("p j c -> p (j c)"), 'bd_st',
                    reads=['bdb'], writes=['bd_d%d' % l])
            if tau >= 1:
                VX(lambda e: e.tensor_tensor(out=B3(yr), in0=B3(Cr), in1=bc16(cur_r), op=ALU.mult))
                VX(lambda e: e.tensor_tensor(out=B3(ta), in0=B3(Ci), in1=bc16(cur_i), op=ALU.mult))
                VX(lambda e: e.tensor_tensor(out=yr, in0=yr, in1=ta, op=ALU.subtract))
                VX(lambda e: e.tensor_tensor(out=B3(yi), in0=B3(Cr), in1=bc16(cur_i), op=ALU.mult))
                VX(lambda e: e.tensor_tensor(out=B3(ta), in0=B3(Ci), in1=bc16(cur_r), op=ALU.mult))
                VX(lambda e: e.tensor_tensor(out=yi, in0=yi, in1=ta, op=ALU.add))
                P.op('dve', lambda e: e.memset(wzb[:, 1024:2048], 0.0), writes=['wzb'])
                for e_ in range(2):
                    P.op('dve', (lambda e_: lambda e: e.tensor_scalar(
                        out=wys[:, :, 0, 16 * e_:16 * e_ + 16], in0=B3(yr), scalar1=masks[:, e_:e_ + 1], scalar2=None,
                        op0=ALU.mult))(e_), reads=[KX, 'masks'], writes=['wzb'])
                    P.op('dve', (lambda e_: lambda e: e.tensor_scalar(
                        out=wys[:, :, 1, 16 * e_:16 * e_ + 16], in0=B3(yi), scalar1=masks[:, e_:e_ + 1], scalar2=-1.0,
                        op0=ALU.mult, op1=ALU.mult))(e_), reads=[KX, 'masks'], writes=['wzb'])
                dma('sp', wy_d[l, tau - 1].rearrange("p a r c -> p (a r c)"), wzb[:, 1024:2048], 'wy_st', reads=['wzb'],
                    writes=['wy_d%d' % l])
            VX(lambda e: e.tensor_tensor(out=nr_, in0=cur_r, in1=lam[:, :, 0, 0], op=ALU.mult), r=['lamp'])
            VX(lambda e: e.tensor_tensor(out=ta[:, 0:16], in0=cur_i, in1=lam[:, :, 0, 1], op=ALU.mult))
            VX(lambda e: e.tensor_tensor(out=nr_, in0=nr_, in1=ta[:, 0:16], op=ALU.subtract))
            VX(lambda e: e.tensor_tensor(out=ni_, in0=cur_r, in1=lam[:, :, 0, 1], op=ALU.mult))
            VX(lambda e: e.tensor_tensor(out=ta[:, 0:16], in0=cur_i, in1=lam[:, :, 0, 0], op=ALU.mult))
            VX(lambda e: e.tensor_tensor(out=ni_, in0=ni_, in1=ta[:, 0:16], op=ALU.add))
            VX(lambda e: e.tensor_copy(out=cur_r, in_=nr_))
            VX(lambda e: e.tensor_copy(out=cur_i, in_=ni_))

    P.barrier()
    if STOP[0] == 's4':
        P.finish(); P.emit(); return

    cur_half = [0]

    def do_tile(kind, t):
        is_p = (kind == 'p')
        NTOK = TT if is_p else 256
        NB = NTOK // 128
        NSEG = 1 if is_p else 4
        SEGL = NTOK // NSEG
        NCH = NTOK // 64
        PAD = 256 if is_p else 32
        SW = PAD + SEGL
        NSTEP = 9 if is_p else 6
        rows = [0] if is_p else [1, 2, 3, 4]
        last_p = is_p and (t == NPT - 1)
        need_out = last_p or (not is_p)
        if not is_p:
            P.op('dve', lambda e: e.memset(upT[:], 0.0), writes=['upT'])

        def seg3(ap2):
            return ap2[:, 0:NSEG * SW].rearrange("p (s w) -> p s w", w=SW)

        def tok3(ap2):
            return ap2.rearrange("p (s w) -> p s w", w=SEGL)

        xsrc = xp[t * TT:(t + 1) * TT, :] if is_p else xs
        for tb in range(NB):
            dma('sp', stg[:], xsrc[tb * 128:(tb + 1) * 128, :], 'stg_ld', writes=['stg'])
            for k4 in range(4):
                def fn(e, k4=k4):
                    ins = None
                    for kk in range(4):
                        kt = k4 * 4 + kk
                        ins = e.transpose(psT[:, kk * 128:(kk + 1) * 128], stg[:, kt * 128:(kt + 1) * 128], identf[:, :])
                    return ins
                P.op('pe', fn, reads=['stg', 'identf'], writes=['psT'])
                P.op('dve', (lambda tb, k4: lambda e: e.tensor_copy(
                    out=xT[:, k4 * 4:(k4 + 1) * 4, tb * 128:(tb + 1) * 128],
                    in_=psT[:, :].rearrange("p (k c) -> p k c", c=128)))(tb, k4),
                    reads=['psT'], writes=['xT'])

        def make_rstd(m, width):
            P.op('dve', lambda e: e.tensor_scalar(out=rstd[:, 0:NTOK], in0=psM[m][:, 0:NTOK], scalar1=1.0 / width,
                                                  scalar2=EPS, op0=ALU.mult, op1=ALU.add),
                 reads=['psM%d' % m], writes=['rstd'])
            P.op('act', lambda e: e.activation(out=rstd[:, 0:NTOK], in_=rstd[:, 0:NTOK], func=AF.Sqrt),
                 reads=['rstd'], writes=['rstd'])
            P.op('dve', lambda e: e.reciprocal(out=rstd[:, 0:NTOK], in_=rstd[:, 0:NTOK]), reads=['rstd'],
                 writes=['rstd'])
        _chk('m1')

        def do_layer(l):
            cur = 1
            prev = 0
            dma('sp', lam[:].rearrange("p a b c -> p (a b c)"), lam_d[l], 'lam_ld', reads=['lam_d%d' % l], writes=['lam'])
            dma('pool', wpl[:], w_pool[l].rearrange("g c d -> c g d"), 'wpl_ld', writes=['wpl'])
            if is_p and t > 0:
                dma('sp', kT[:, :, 0:512], kh_d[l], 'kh_ld', reads=['kh_d%d' % l], writes=['kTh'])
                dma('sp', vb[:, 0:4, :], vh_d[l], 'vh_ld', reads=['vh_d%d' % l], writes=['vbh'])
                P.op('dve', lambda e: e.tensor_copy(out=upT[:, :, 0:15], in_=uph[:, l, :, :]), reads=['uph'], writes=['upT'])

            m = next_psM()
            for kt in range(KT):
                qi = rot('sqb', 2)
                P.op('act', (lambda kt, qi: lambda e: e.activation(out=sqb[qi][:, 0:NTOK], in_=xT[:, kt, 0:NTOK],
                                                                   func=AF.Square))(kt, qi),
                     reads=['xT'], writes=['sqb%d' % qi])
                P.op('pe', (lambda kt, qi, m: lambda e: e.matmul(psM[m][:, 0:NTOK], onesb[:, :], sqb[qi][:, 0:NTOK],
                                                                 start=(kt == 0), stop=(kt == KT - 1)))(kt, qi, m),
                     reads=['sqb%d' % qi, 'onesb'], writes=['psM%d' % m])

            make_rstd(m, D)
            for kt in range(KT):
                ti = rot('tmpf', 2)
                for sg in range(NSEG):
                    r = rows[sg]
                    c0, c1 = sg * SEGL, (sg + 1) * SEGL
                    P.op('dve', (lambda kt, ti, r, c0, c1: lambda e: e.scalar_tensor_tensor(
                        out=tmpf[ti][:, c0:c1], in0=xT[:, kt, c0:c1], scalar=Gm[:, l, kt, r:r + 1],
                        in1=rstd[:, c0:c1], op0=ALU.mult, op1=ALU.mult))(kt, ti, r, c0, c1),
                        reads=['xT', 'Gm', 'rstd'], writes=['tmpf%d' % ti])
                    P.op('act', (lambda kt, ti, r, c0, c1: lambda e: e.activation(
                        out=hy[:, kt, c0:c1], in_=tmpf[ti][:, c0:c1], func=AF.Identity,
                        bias=modT[:, l, kt, r:r + 1], scale=1.0))(kt, ti, r, c0, c1),
                        reads=['tmpf%d' % ti, 'modT'], writes=['hy'])

            _chk('m2')
            if not is_p:
                for sg in range(4):
                    dma('sp', stg[0:15, 0:512], spool[l, sg], 'stg_ld', writes=['stg'])

                    def fn(e):
                        ins = None
                        for g in range(4):
                            ins = e.transpose(psT[:, g * 16:g * 16 + 15], stg[0:15, g * 128:(g + 1) * 128],
                                              identf[0:15, 0:15])
                        return ins
                    P.op('pe', fn, reads=['stg', 'identf'], writes=['psT'])
                    P.op('dve', (lambda sg: lambda e: e.tensor_copy(
                        out=upT[:, :, sg * 79:sg * 79 + 15],
                        in_=psT[:, 0:64].rearrange("p (g c) -> p g c", c=16)[:, :, 0:15]))(sg),
                        reads=['psT'], writes=['upT'])
                    for e_ in range(2):
                        dma('sp', stR[64 * e_:64 * e_ + 64, l, :, sg],
                            bass.AP(sre.tensor, ((l * 4 + sg) * 32 + e_) * 64, [[1, 64], [128, 16]]),
                            'st_ld', writes=['stR'], nonc=True)
                        dma('sp', stI[64 * e_:64 * e_ + 64, l, :, sg],
                            bass.AP(sim.tensor, ((l * 4 + sg) * 32 + e_) * 64, [[1, 64], [128, 16]]),
                            'st_ld', writes=['stI'], nonc=True)

            def ext_cols(sg):
                if is_p:
                    return cur * 512, cur * 512 + 512
                return 512 + 128 * sg, 512 + 128 * sg + 64

            def proj_fm(b, m):
                mm_group(psM[m][:, 0:NTOK], [(wblk[b][:, kt, :], hy[:, kt, 0:NTOK]) for kt in range(KT)],
                         reads=['wblk%d' % b, 'hy'], writes=['psM%d' % m])

            def proj_tm(b, j, dst_bf_fn, odram):
                for tb in range(NB):
                    m = next_psM()
                    mm_group(psM[m][:, 0:128],
                             [(hy[:, kt, tb * 128:(tb + 1) * 128], wblk[b][:, kt, :]) for kt in range(KT)],
                             reads=['wblk%d' % b, 'hy'], writes=['psM%d' % m])
                    if dst_bf_fn is not None:
                        dst_bf_fn(tb, m)
                    if odram is not None:
                        ki = rot('kvs', 4)
                        P.op('act', (lambda m, ki: lambda e: e.copy(out=kvs[ki][:, :], in_=psM[m][:, 0:128]))(m, ki),
                             reads=['psM%d' % m], writes=['kvs%d' % ki])
                        dma('sp', odram[l, tb * 128:(tb + 1) * 128, j * 128:(j + 1) * 128], kvs[ki][:, :],
                            'kvs_o%d' % ki, reads=['kvs%d' % ki], writes=['o_kv'], out=True)

            for cb in IN_ORDER:
                b = w_next()
                if 8 <= cb < 16:
                    j = cb - 8
                    m = next_psM(); proj_fm(b, m)
                    for sg in range(NSEG):
                        c0, c1 = ext_cols(sg)
                        P.op('dve', (lambda j, m, sg, c0, c1: lambda e: e.tensor_copy(
                            out=kT[:, j, c0:c1], in_=psM[m][:, sg * SEGL:(sg + 1) * SEGL]))(j, m, sg, c0, c1),
                            reads=['psM%d' % m], writes=['kTn'])
                    if need_out:
                        proj_tm(b, j, None, nkp if is_p else nks)
                elif 16 <= cb < 24:
                    j = cb - 16

                    def vdst(tb, m, j=j):
                        if is_p:
                            P.op('dve', lambda e: e.tensor_copy(out=vb[:, cur * 4 + tb, j * 128:(j + 1) * 128],
                                                                in_=psM[m][:, 0:128]),
                                 reads=['psM%d' % m], writes=['vbn'])
                        else:
                            for hh in range(2):
                                sg = 2 * tb + hh
                                P.op('dve', (lambda sg, hh: lambda e: e.tensor_copy(
                                    out=vb[0:64, 4 + sg, j * 128:(j + 1) * 128],
                                    in_=psM[m][64 * hh:64 * hh + 64, 0:128]))(sg, hh),
                                    reads=['psM%d' % m], writes=['vbn'])
                    if need_out:
                        proj_tm(b, j, vdst, (nvp if is_p else nvs))
                    else:
                        m = next_psM(); proj_fm(b, m)
                        P.op('act', (lambda m: lambda e: e.copy(out=sqb[1][:, 0:NTOK], in_=psM[m][:, 0:NTOK]))(m),
                             reads=['psM%d' % m], writes=['sqb1'])

                        def fn_vt(e):
                            ins = None
                            for tb in range(NB):
                                ins = e.transpose(psTb[:, tb * 128:(tb + 1) * 128], sqb[1][:, tb * 128:(tb + 1) * 128],
                                                  identb[:, :])
                            return ins
                        P.op('pe', fn_vt, reads=['sqb1', 'identb'], writes=['psTb'])
                        P.op('dve', (lambda j: lambda e: e.tensor_copy(
                            out=vb[:, 4:4 + NB, j * 128:(j + 1) * 128],
                            in_=psTb[:, 0:NB * 128].rearrange("p (b c) -> p b c", c=128)))(j),
                            reads=['psTb'], writes=['vbn'])
                elif 32 <= cb < 36:
                    j = cb - 32
                    m = next_psM(); proj_fm(b, m)
                    P.op('act', (lambda j, m: lambda e: e.copy(out=ub[:, j, 0:NTOK], in_=psM[m][:, 0:NTOK]))(j, m),
                         reads=['psM%d' % m], writes=['ub'])
                elif 40 <= cb < 44:
                    g = cb - 40
                    m = next_psM(); proj_fm(b, m)
                    P.op('act', (lambda g, m: lambda e: e.copy(
                        out=upT[:, g, 0:NSEG * (15 + SEGL)].rearrange("p (s w) -> p s w", w=15 + SEGL)[:, :, 15:15 + SEGL],
                        in_=tok3(psM[m][:, 0:NTOK])))(g, m),
                        reads=['psM%d' % m], writes=['upT'])
                elif cb < 8:
                    j = cb
                    m = next_psM(); proj_fm(b, m)
                    P.op('dve', (lambda j, m: lambda e: e.tensor_copy(out=qT[:, j, 0:NTOK], in_=psM[m][:, 0:NTOK]))(j, m),
                         reads=['psM%d' % m], writes=['qT'])
                else:
                    if 24 <= cb < 32:
                        zi = cb - 24
                    elif 36 <= cb < 40:
                        zi = 8 + cb - 36
                    else:
                        zi = 12 + cb - 44
                    m = next_psM(); proj_fm(b, m)
                    P.op('act', (lambda zi, m: lambda e: e.activation(out=zT[:, zi, 0:NTOK], in_=psM[m][:, 0:NTOK],
                                                                      func=AF.Silu))(zi, m),
                         reads=['psM%d' % m], writes=['zT'])

            _chk('m3')
            if is_p and not last_p:
                dma('sp', kh_d[l], kT[:, :, 512:1024], 'kh_st', reads=['kTn'], writes=['kh_d%d' % l])
                dma('sp', vh_d[l], vb[:, 4:8, :], 'vh_st', reads=['vbn'], writes=['vh_d%d' % l])

            def band_pieces(mc):
                if not is_p:
                    return [(p, 0, 128, p) for p in range(4)] + [(4 + mc, 0, 64, 4)]
                out_ = []
                if mc % 2 == 0:
                    for p in range(5):
                        c0 = mc - 8 + 2 * p
                        if c0 < 0 and t == 0:
                            continue
                        if p == 4:
                            out_.append((cur * 4 + mc // 2, 0, 64, 4))
                        elif c0 < 0:
                            out_.append((prev * 4 + (c0 + 8) // 2, 0, 128, p))
                        else:
                            out_.append((cur * 4 + c0 // 2, 0, 128, p))
                else:
                    for p in range(5):
                        c1 = mc - 8 + 2 * p
                        c0 = c1 - 1
                        if p == 0:
                            if t == 0:
                                continue
                            out_.append((prev * 4 + (c1 + 8) // 2, 64, 128, 0))
                        elif c0 < 0:
                            if t == 0:
                                continue
                            out_.append((prev * 4 + (c0 + 8) // 2, 0, 128, p))
                        else:
                            out_.append((cur * 4 + c0 // 2, 0, 128, p))
                return out_

            def load_hist(sg):
                dma('pool', vb[:, 0:4, :], cv[l, sg].rearrange("(b p) f -> p b f", p=128), 'vb_ld', writes=['vbh'])
                for rb in range(4):
                    ci = rot("ckb", 1)
                    dma('pool', ckb[ci][:], ck[l, sg, rb * 128:(rb + 1) * 128, :], 'ckb_ld%d' % ci, writes=['ckb%d' % ci])

                    def fn(e, ci=ci):
                        ins = None
                        for ft in range(8):
                            ins = e.transpose(psTb[:, ft * 128:(ft + 1) * 128], ckb[ci][:, ft * 128:(ft + 1) * 128],
                                              identb[:, :])
                        return ins
                    P.op('pe', fn, reads=['ckb%d' % ci, 'identb'], writes=['psTb'])
                    P.op('dve', (lambda rb: lambda e: e.tensor_copy(
                        out=kT[:, :, rb * 128:(rb + 1) * 128],
                        in_=psTb[:, :].rearrange("p (f c) -> p f c", c=128)))(rb),
                        reads=['psTb'], writes=['kTh'])

            def attn_unit(j, mc, tokc0):
                pcs = band_pieces(mc)
                ei = rot('ef', 2)

                def fn_s(e):
                    ins = None
                    for (blk, lo, hi, pp) in pcs:
                        for hh in range(2):
                            ins = e.matmul(psA[:, 512 * hh + pp * 64:512 * hh + pp * 64 + 64],
                                           kT[64 * hh:64 * hh + 64, j, blk * 128:(blk + 1) * 128],
                                           qT[64 * hh:64 * hh + 64, j, tokc0:tokc0 + 64], start=True, stop=True,
                                           tile_position=(64 * hh, 0))
                    return ins
                P.op('pe', fn_s, reads=['kTh', 'kTn', 'qT'], writes=['psA'])
                p_lo = min(pp for (_, _, _, pp) in pcs)
                p_hi = max(pp for (_, _, _, pp) in pcs) + 1
                P.op('act', lambda e: e.activation(
                    out=ef[ei][:, p_lo * 128:p_hi * 128].rearrange("p (q h i) -> p q h i", h=2, i=64),
                    in_=psA[:, :].rearrange("p (h q i) -> p q h i", h=2, i=64)[:, p_lo:p_hi, :, :],
                    func=AF.Exp, scale=0.125),
                     reads=['psA'], writes=['ef%d' % ei])
                P.op('dve', lambda e: e.tensor_tensor(out=pt16[ei][:, p_lo * 128:p_hi * 128],
                                                      in0=ef[ei][:, p_lo * 128:p_hi * 128],
                                                      in1=ebt[:, j, p_lo * 128:p_hi * 128], op=ALU.mult),
                     reads=['ef%d' % ei, 'ebt'], writes=['pt16_%d' % ei])
                n = len(pcs)

                def fn_o(e):
                    ins = None
                    for dst0, use_v in ((0, True), (128, False)):
                        for i, (blk, lo, hi, pp) in enumerate(pcs):
                            lhs = vb[:, blk, j * 128:(j + 1) * 128] if use_v else onesb[:, :]
                            ins = e.matmul(psO[:, dst0:dst0 + 128], lhs, pt16[ei][:, pp * 128:(pp + 1) * 128],
                                           start=(i == 0), stop=(i == n - 1))
                    return ins
                P.op('pe', fn_o, reads=['vbh', 'vbn', 'pt16_%d' % ei, 'onesb'], writes=['psO'])
                P.op('dve', lambda e: e.reciprocal(out=rden[ei][:, :], in_=psO[:, 128:256]), reads=['psO'],
                     writes=['rden%d' % ei])
                for hh in range(2):
                    P.op('dve', (lambda hh: lambda e: e.tensor_tensor(
                        out=hy[64 * hh:64 * hh + 64, j, tokc0:tokc0 + 64],
                        in0=psO[64 * hh:64 * hh + 64, 64 * hh:64 * hh + 64],
                        in1=rden[ei][64 * hh:64 * hh + 64, 64 * hh:64 * hh + 64], op=ALU.mult))(hh),
                        reads=['psO', 'rden%d' % ei], writes=['hy'])

            NCS = SEGL // 16
            NC16 = NTOK // 16
            KSTEPS = 5 if is_p else 2

            def zseg(buf):
                return buf[:, :, 0:NC16].rearrange("p a (s c) -> p a s c", c=NCS)

            def ustr(j, s_, plo=0, phi=128):
                return ub[plo:phi, j, 0:NTOK].rearrange("p (n s) -> p n s", s=16)[:, :, s_]

            def ssm_Z():
                for j in range(4):
                    dma('sp', wzb[:].rearrange("p (s x) -> p s x", s=16),
                        wz_d[l, :, :, j, :, :].rearrange("s p r c -> p s (r c)"), 'wzb_ld', reads=['wz_d%d' % l],
                        writes=['wzb'])
                    wz4 = wzb[:].rearrange("p (s r c) -> p s r c", r=2, c=128)
                    for q in range(4):
                        pr = 4 * j + q
                        m = next_psM()
                        for ri in range(2):
                            def fn(e, q=q, ri=ri, m=m, j=j):
                                ins = None
                                for s_ in range(16):
                                    ins = e.matmul(psM[m][:, ri * NC16:(ri + 1) * NC16], wz4[32 * q:32 * q + 32, s_, ri, :],
                                                   ustr(j, s_, 32 * q, 32 * q + 32), start=(s_ == 0), stop=(s_ == 15),
                                                   tile_position=(32 * q, 0))
                                return ins
                            P.op('pe', fn, reads=['wzb', 'ub'], writes=['psM%d' % m])
                        P.op('act', (lambda pr, m: lambda e: e.copy(out=Zr[:, pr, 0:NC16], in_=psM[m][:, 0:NC16]))(pr, m),
                             reads=['psM%d' % m], writes=['Zr'])
                        P.op('act', (lambda pr, m: lambda e: e.copy(out=Zi[:, pr, 0:NC16], in_=psM[m][:, NC16:2 * NC16]))(pr, m),
                             reads=['psM%d' % m], writes=['Zi'])

            def chunk_scan():
                def co(k, c, shape):
                    a_ = lam[:, :, 4 + k, c]
                    return a_.unsqueeze(2).unsqueeze(3).broadcast_to(shape)
                t1 = rstd[:, 0:512].rearrange("p (a c) -> p a c", c=32)
                t2 = ef[0][:, 0:512].rearrange("p (a c) -> p a c", c=32)
                X = {'Zr': Zr, 'Zi': Zi, 'Pr': tmpf[0][:, 0:512].rearrange("p (a c) -> p a c", c=32),
                     'Pi': tmpf[1][:, 0:512].rearrange("p (a c) -> p a c", c=32)}
                KEY = {'Zr': 'Zr', 'Zi': 'Zi', 'Pr': 'tmpf0', 'Pi': 'tmpf1'}

                def TT(out_, in0_, in1_, op_, r, w):
                    P.op('dve', lambda e: e.tensor_tensor(out=out_, in0=in0_, in1=in1_, op=op_), reads=r, writes=w)
                sh1 = [128, 16, NSEG, 1]
                fr = zseg(Zr)[:, :, :, 0:1]; fi = zseg(Zi)[:, :, :, 0:1]
                cr = stR[:, l, :, 0:NSEG].unsqueeze(3); ci = stI[:, l, :, 0:NSEG].unsqueeze(3)
                u1 = zseg(t1)[:, :, :, 0:1]
                TT(u1, cr, co(0, 0, sh1), ALU.mult, ['stR', 'lam'], ['rstd'])
                TT(fr, fr, u1, ALU.add, ['rstd', 'Zr'], ['Zr'])
                TT(u1, ci, co(0, 1, sh1), ALU.mult, ['stI', 'lam', 'Zr'], ['rstd'])
                TT(fr, fr, u1, ALU.subtract, ['rstd', 'Zr'], ['Zr'])
                TT(u1, ci, co(0, 0, sh1), ALU.mult, ['stI', 'lam', 'Zr'], ['rstd'])
                TT(fi, fi, u1, ALU.add, ['rstd', 'Zi'], ['Zi'])
                TT(u1, cr, co(0, 1, sh1), ALU.mult, ['stR', 'lam', 'Zi'], ['rstd'])
                TT(fi, fi, u1, ALU.add, ['rstd', 'Zi'], ['Zi'])
                ar_, ai_, br_, bi_ = 'Zr', 'Zi', 'Pr', 'Pi'
                for k in range(KSTEPS):
                    s_ = 1 << k
                    w_ = NCS - s_
                    shp = [128, 16, NSEG, w_]
                    A_r = zseg(X[ar_]); A_i = zseg(X[ai_]); B_r = zseg(X[br_]); B_i = zseg(X[bi_])
                    v1 = zseg(t1)[:, :, :, 0:w_]; v2 = zseg(t2)[:, :, :, 0:w_]
                    TT(v1, A_r[:, :, :, 0:w_], co(k, 0, shp), ALU.mult, [KEY[ar_], 'lam'], ['rstd'])
                    TT(v2, A_i[:, :, :, 0:w_], co(k, 1, shp), ALU.mult, [KEY[ai_], 'lam'], ['ef0'])
                    TT(B_r[:, :, :, s_:NCS], A_r[:, :, :, s_:NCS], v1, ALU.add, [KEY[ar_], 'rstd'], [KEY[br_]])
                    TT(B_r[:, :, :, s_:NCS], B_r[:, :, :, s_:NCS], v2, ALU.subtract, [KEY[br_], 'ef0'], [KEY[br_]])
                    TT(B_r[:, :, :, 0:s_], A_r[:, :, :, 0:s_], A_r[:, :, :, 0:s_], ALU.max, [KEY[ar_]], [KEY[br_]])
                    TT(v1, A_i[:, :, :, 0:w_], co(k, 0, shp), ALU.mult, [KEY[ai_], 'lam', KEY[br_]], ['rstd'])
                    TT(v2, A_r[:, :, :, 0:w_], co(k, 1, shp), ALU.mult, [KEY[ar_], 'lam', KEY[br_]], ['ef0'])
                    TT(B_i[:, :, :, s_:NCS], A_i[:, :, :, s_:NCS], v1, ALU.add, [KEY[ai_], 'rstd'], [KEY[bi_]])
                    TT(B_i[:, :, :, s_:NCS], B_i[:, :, :, s_:NCS], v2, ALU.add, [KEY[bi_], 'ef0'], [KEY[bi_]])
                    TT(B_i[:, :, :, 0:s_], A_i[:, :, :, 0:s_], A_i[:, :, :, 0:s_], ALU.max, [KEY[ai_]], [KEY[bi_]])
                    ar_, br_ = br_, ar_
                    ai_, bi_ = bi_, ai_
                S_r = zseg(X[ar_]); S_i = zseg(X[ai_])
                for ri, S_, cc, kk in ((0, S_r, cr, KEY[ar_]), (1, S_i, ci, KEY[ai_])):
                    spv = Sp[:, ri, :, 0:NC16].rearrange("p a (s c) -> p a s c", c=NCS)
                    P.op('dve', (lambda spv, cc: lambda e: e.tensor_copy(out=spv[:, :, :, 0:1], in_=cc))(spv, cc),
                         reads=['stR', 'stI'], writes=['Sp'])
                    P.op('dve', (lambda spv, S_: lambda e: e.tensor_copy(out=spv[:, :, :, 1:NCS], in_=S_[:, :, :, 0:NCS - 1]))(spv, S_),
                         reads=[kk], writes=['Sp'])
                P.op('dve', lambda e: e.tensor_copy(out=cr, in_=S_r[:, :, :, NCS - 1:NCS]), reads=[KEY[ar_], 'Sp'], writes=['stR'])
                P.op('dve', lambda e: e.tensor_copy(out=ci, in_=S_i[:, :, :, NCS - 1:NCS]), reads=[KEY[ai_], 'Sp'], writes=['stI'])

            def ssm_Y():
                for j in range(4):
                    dma('sp', bdb[:], bd_d[l, :, :, j, :].rearrange("t p c -> p t c"), 'bdb_ld', reads=['bd_d%d' % l],
                        writes=['bdb'])
                    dma('sp', wzb[:].rearrange("p (t x) -> p t x", t=16),
                        wy_d[l, :, :, 4 * j:4 * j + 4, :, :].rearrange("t p q r c -> p t (q r c)"), 'wzb_ld',
                        reads=['wy_d%d' % l], writes=['wzb'])
                    wy5 = wzb[:].rearrange("p (t q r c) -> p t q r c", q=4, r=2, c=32)

                    def fn(e, j=j):
                        ins = None
                        for t_ in range(16):
                            o_ = psT[:, 0:NTOK].rearrange("p (n s) -> p n s", s=16)[:, :, t_]
                            for s_ in range(t_ + 1):
                                ins = e.matmul(o_, bdb[:, t_ - s_, :], ustr(j, s_), start=(s_ == 0), stop=False)
                            for q in range(4):
                                oq = psT[32 * q:32 * q + 32, 0:NTOK].rearrange("p (n s) -> p n s", s=16)[:, :, t_]
                                for ri in range(2):
                                    ins = e.matmul(oq, wy5[:, t_, q, ri, :], Sp[:, ri, 4 * j + q, 0:NC16], start=False,
                                                   stop=(ri == 1), tile_position=(0, 32 * q))
                        return ins
                    P.op('pe', fn, reads=['bdb', 'wzb', 'ub', 'Sp'], writes=['psT'])
                    ti = rot('tmpf', 2)
                    P.op('dve', (lambda j, ti: lambda e: e.scalar_tensor_tensor(
                        out=tmpf[ti][:, 0:NTOK], in0=ub[:, j, 0:NTOK], scalar=dskT[:, l, j:j + 1], in1=psT[:, 0:NTOK],
                        op0=ALU.mult, op1=ALU.add))(j, ti),
                        reads=['ub', 'psT'], writes=['tmpf%d' % ti])
                    P.op('act', (lambda j, ti: lambda e: e.activation(out=gT[:, j, 0:NTOK], in_=tmpf[ti][:, 0:NTOK],
                                                                      func=AF.Gelu))(j, ti),
                         reads=['tmpf%d' % ti], writes=['gT'])

            def ssm_glu():
                for c4 in range(4):
                    b = w_next()
                    ma = next_psM()
                    mm_group(psM[ma][:, 0:NTOK], [(wblk[b][:, kt, :], gT[:, kt, 0:NTOK]) for kt in range(4)],
                             reads=['wblk%d' % b, 'gT'], writes=['psM%d' % ma])
                    P.op('act', (lambda c4, ma: lambda e: e.activation(out=tmpf[0][:, 0:NTOK], in_=psM[ma][:, 0:NTOK],
                                                                       func=AF.Identity, bias=bgluT[:, l, c4:c4 + 1],
                                                                       scale=1.0))(c4, ma),
                         reads=['psM%d' % ma], writes=['tmpf0'])
                    ms = next_psM()
                    mm_group(psM[ms][:, 0:NTOK],
                             [(wblk[b][:, 4 + kt, :], gT[:, kt, 0:NTOK]) for kt in range(4)],
                             reads=['wblk%d' % b, 'gT'], writes=['psM%d' % ms])
                    P.op('act', (lambda c4, ms: lambda e: e.activation(out=tmpf[1][:, 0:NTOK], in_=psM[ms][:, 0:NTOK],
                                                                       func=AF.Sigmoid, bias=bgluT[:, l, 4 + c4:5 + c4],
                                                                       scale=1.0))(c4, ms),
                         reads=['psM%d' % ms], writes=['tmpf1'])
                    P.op('dve', (lambda c4: lambda e: e.tensor_tensor(out=hy[:, 8 + c4, 0:NTOK], in0=tmpf[0][:, 0:NTOK],
                                                                      in1=tmpf[1][:, 0:NTOK], op=ALU.mult))(c4),
                         reads=['tmpf0', 'tmpf1'], writes=['hy'])

            def pool_branch():
                EW = 15 + SEGL

                def up3(g):
                    return upT[:, g, 0:NSEG * EW].rearrange("p (s w) -> p s w", w=EW)

                def pt3(bufap):
                    return bufap[:, 0:NSEG * EW].rearrange("p (s w) -> p s w", w=EW)
                for g, w in enumerate((2, 4, 8, 16)):
                    src3 = up3(g)
                    srck = 'upT'
                    nst = g + 1
                    tbufs = [(pta, 'pta'), (ptb, 'ptb')]
                    for si_ in range(nst):
                        sh = 1 << si_
                        lo = 15 - (w - (sh << 1))
                        dst, dstk = tbufs[si_ % 2]
                        P.op('pool', (lambda dst, src3, lo, sh: lambda e: e.tensor_tensor(
                            out=pt3(dst)[:, :, lo:EW], in0=src3[:, :, lo:EW], in1=src3[:, :, lo - sh:EW - sh],
                            op=ALU.add))(dst, src3, lo, sh),
                            reads=[srck], writes=[dstk])
                        src3, srck = pt3(dst), dstk
                    P.op('dve', (lambda src3, g, w: lambda e: e.scalar_tensor_tensor(
                        out=tok3(sqb[0][:, 0:NTOK]), in0=src3[:, :, 15:EW], scalar=1.0 / w, in1=up3(g)[:, :, 15:EW],
                        op0=ALU.mult, op1=ALU.subtract))(src3, g, w),
                        reads=[srck, 'upT'], writes=['sqb0'])
                    if is_p and t == 0:
                        P.op('dve', (lambda src3, g: lambda e: e.tensor_tensor(
                            out=tmpf[0][:, 0:16], in0=src3[:, 0, 15:31], in1=invcnt[:, g, :], op=ALU.mult))(src3, g),
                            reads=[srck, 'invcnt'], writes=['tmpf0'])
                        P.op('dve', (lambda g: lambda e: e.tensor_tensor(
                            out=sqb[0][:, 0:16], in0=tmpf[0][:, 0:16], in1=upT[:, g, 15:31], op=ALU.subtract))(g),
                            reads=['tmpf0', 'upT'], writes=['sqb0'])
                    mpl = next_psM()
                    P.op('pe', (lambda g, mpl: lambda e: e.matmul(psM[mpl][:, 0:NTOK], wpl[:, g, :], sqb[0][:, 0:NTOK],
                                                                  start=True, stop=True))(g, mpl),
                         reads=['wpl', 'sqb0'], writes=['psM%d' % mpl])
                    P.op('act', (lambda g, mpl: lambda e: e.activation(out=hy[:, 12 + g, 0:NTOK], in_=psM[mpl][:, 0:NTOK],
                                                                       func=AF.Identity, scale=pscT[:, l, g:g + 1]))(g, mpl),
                         reads=['psM%d' % mpl], writes=['hy'])
                if need_out:
                    for sg in range(NSEG):
                        def fn(e, sg=sg):
                            ins = None
                            for g in range(4):
                                ins = e.transpose(psT[0:15, g * 128:(g + 1) * 128], up3(g)[:, sg, EW - 15:EW], identf[:, :])
                            return ins
                        P.op('pe', fn, reads=['upT', 'identf'], writes=['psT'])
                        P.op('dve', lambda e: e.tensor_copy(out=tmpf[1][0:15, :], in_=psT[0:15, :]), reads=['psT'],
                             writes=['tmpf1'])
                        dst = npp[l] if is_p else nps[l, sg]
                        dma('sp', dst, tmpf[1][0:15, :], 'pst_o', reads=['tmpf1'], writes=['o_pool'], out=True)
                if is_p and not last_p:
                    P.op('pool', lambda e: e.tensor_copy(out=uph[:, l, :, :], in_=upT[:, :, EW - 15:EW]),
                         reads=['upT'], writes=['uph'])

            def load_ebt(par):
                dma('sp', ebt[:].rearrange("p a c -> p (a c)"), eb_d[l, par], 'ebt_ld',
                    reads=['eb_d%d_%d' % (l, par)], writes=['ebt'])

            units = []
            if is_p:
                for par in range(2):
                    units.append(('ebt', par))
                    for j in range(8):
                        for mc in range(par, NCH, 2):
                            units.append(('att', j, mc, mc * 64))
            else:
                units.append(('ebt', 0))
                for sg in range(4):
                    units.append(('hist', sg))
                    for j in range(8):
                        units.append(('att', j, sg, sg * 64))
            ssm_Z()
            chunk_scan()
            ssm_Y()
            for u in units:
                if u[0] == 'ebt':
                    load_ebt(u[1])
                elif u[0] == 'hist':
                    load_hist(u[1])
                else:
                    attn_unit(u[1], u[2], u[3])
            _chk('m4')
            pool_branch()
            ssm_glu()
            _chk('m5')

            if DEBUG[0] and is_p and t == 0 and l == 0:
                dbg = nc.dram_tensor("dbg_hy", [128, KT * TT], BF16, kind="ExternalOutput").ap()
                dma('sp', dbg, hy[:].rearrange("p k c -> p (k c)"), 'dbg_o', reads=['hy'], writes=['o_dbg'], out=True)
                dbg3 = nc.dram_tensor("dbg_t0", [128, TT], F32, kind="ExternalOutput").ap()
                dma('sp', dbg3, tmpf[0][:], 'dbg_o', reads=['tmpf0'], writes=['o_dbg'], out=True)
                dbg4 = nc.dram_tensor("dbg_t1", [128, TT], F32, kind="ExternalOutput").ap()
                dma('sp', dbg4, tmpf[1][:], 'dbg_o', reads=['tmpf1'], writes=['o_dbg'], out=True)
                dbg2 = nc.dram_tensor("dbg_g", [128, 4 * TT], BF16, kind="ExternalOutput").ap()
                dma('sp', dbg2, gT[:].rearrange("p k c -> p (k c)"), 'dbg_o', reads=['gT'], writes=['o_dbg'], out=True)

            if need_out:
                for sg in range(NSEG):
                    for e_ in range(2):
                        dr = (nsrp[l] if is_p else nsrs[l, sg])
                        di = (nsip[l] if is_p else nsis[l, sg])
                        dma('sp', bass.AP(dr.tensor, dr.offset + e_ * 64, [[1, 64], [128, 16]]),
                            stR[64 * e_:64 * e_ + 64, l, :, sg], 'st_o', reads=['stR'], writes=['o_st'], out=True, nonc=True)
                        dma('sp', bass.AP(di.tensor, di.offset + e_ * 64, [[1, 64], [128, 16]]),
                            stI[64 * e_:64 * e_ + 64, l, :, sg], 'st_o', reads=['stI'], writes=['o_st'], out=True, nonc=True)

            for (f0, f1, width) in ((0, 8, 1024), (8, 12, 512), (12, 16, 512)):
                m = next_psM()
                for f in range(f0, f1):
                    qi = rot('sqb', 2)
                    P.op('act', (lambda f, qi: lambda e: e.activation(out=sqb[qi][:, 0:NTOK], in_=hy[:, f, 0:NTOK],
                                                                      func=AF.Square))(f, qi),
                         reads=['hy'], writes=['sqb%d' % qi])
                    P.op('pe', (lambda f, qi, m, f0, f1: lambda e: e.matmul(psM[m][:, 0:NTOK], onesb[:, :], sqb[qi][:, 0:NTOK],
                                                                            start=(f == f0), stop=(f == f1 - 1)))(f, qi, m, f0, f1),
                         reads=['sqb%d' % qi, 'onesb'], writes=['psM%d' % m])
                make_rstd(m, width)
                for f in range(f0, f1):
                    ti = rot('tmpf', 2)
                    P.op('dve', (lambda f, ti: lambda e: e.scalar_tensor_tensor(
                        out=tmpf[ti][:, 0:NTOK], in0=hy[:, f, 0:NTOK], scalar=bngT[:, l, f:f + 1], in1=rstd[:, 0:NTOK],
                        op0=ALU.mult, op1=ALU.mult))(f, ti),
                        reads=['hy', 'rstd'], writes=['tmpf%d' % ti])
                    P.op('dve', (lambda f, ti: lambda e: e.tensor_tensor(
                        out=hy[:, f, 0:NTOK], in0=tmpf[ti][:, 0:NTOK], in1=zT[:, f, 0:NTOK], op=ALU.mult))(f, ti),
                        reads=['tmpf%d' % ti, 'zT'], writes=['hy'])

            _chk('m6')
            for cb in range(16):
                b = w_next()
                m = next_psM()
                mm_group(psM[m][:, 0:NTOK], [(wblk[b][:, kt, :], hy[:, kt, 0:NTOK]) for kt in range(KT)],
                         reads=['wblk%d' % b, 'hy'], writes=['psM%d' % m])
                for sg in range(NSEG):
                    r = rows[sg]
                    c0, c1 = sg * SEGL, (sg + 1) * SEGL
                    P.op('dve', (lambda cb, m, r, c0, c1: lambda e: e.scalar_tensor_tensor(
                        out=xT[:, cb, c0:c1], in0=psM[m][:, c0:c1], scalar=modT[:, l, 32 + cb, r:r + 1],
                        in1=xT[:, cb, c0:c1], op0=ALU.mult, op1=ALU.add))(cb, m, r, c0, c1),
                        reads=['psM%d' % m, 'modT', 'xT'], writes=['xT'])

        for l_ in range(NLR):
            do_layer(l_)
        _chk('m7')

        m = next_psM()
        for kt in range(KT):
            qi = rot('sqb', 2)
            P.op('act', (lambda kt, qi: lambda e: e.activation(out=sqb[qi][:, 0:NTOK], in_=xT[:, kt, 0:NTOK],
                                                               func=AF.Square))(kt, qi),
                 reads=['xT'], writes=['sqb%d' % qi])
            P.op('pe', (lambda kt, qi, m: lambda e: e.matmul(psM[m][:, 0:NTOK], onesb[:, :], sqb[qi][:, 0:NTOK],
                                                             start=(kt == 0), stop=(kt == KT - 1)))(kt, qi, m),
                 reads=['sqb%d' % qi, 'onesb'], writes=['psM%d' % m])
        make_rstd(m, D)
        for kt in range(KT):
            P.op('dve', (lambda kt: lambda e: e.scalar_tensor_tensor(
                out=xT[:, kt, 0:NTOK], in0=xT[:, kt, 0:NTOK], scalar=fngT[:, kt:kt + 1], in1=rstd[:, 0:NTOK],
                op0=ALU.mult, op1=ALU.mult))(kt),
                reads=['xT', 'rstd'], writes=['xT'])
        ydst = yp[t * TT:(t + 1) * TT, :] if is_p else ys
        for tb in range(NB):
            for k4 in range(4):
                def fn(e, tb=tb, k4=k4):
                    ins = None
                    for kk in range(4):
                        kt = k4 * 4 + kk
                        ins = e.transpose(psT[:, kk * 128:(kk + 1) * 128], xT[:, kt, tb * 128:(tb + 1) * 128], identf[:, :])
                    return ins
                P.op('pe', fn, reads=['xT', 'identf'], writes=['psT'])
                P.op('act', (lambda k4: lambda e: e.copy(out=stg[:, k4 * 512:(k4 + 1) * 512], in_=psT[:, :]))(k4),
                     reads=['psT'], writes=['stg'])
            dma('sp', ydst[tb * 128:(tb + 1) * 128, :], stg[:], 'stg_o', reads=['stg'], writes=['o_y'], out=True)

    try:
        for (kind_, t_) in tiles:
            do_tile(kind_, t_)
    except _Stop:
        pass
    P.finish()
    P.emit()


def _consts():
    identf = np.eye(128, dtype=np.float32)
    j64 = np.ascontiguousarray(np.eye(64, dtype=np.float32)[::-1])
    invcnt = np.zeros((128, 4, 16), np.float32)
    for g, w in enumerate((2, 4, 8, 16)):
        for tt in range(16):
            invcnt[:, g, tt] = 1.0 / min(w, tt + 1)
    masks = np.zeros((128, 2), np.float32)
    masks[0:64, 0] = 1.0
    masks[64:128, 1] = 1.0
    bmask = np.kron(np.eye(4, dtype=np.float32), np.ones((32, 32), np.float32))
    return identf, j64, invcnt, masks, bmask


_NC_CACHE = {}


def _in_maps(inp, NPT, n_cores):
    identf, j64, invcnt, masks, bmask = _consts()
    f = lambda a: np.ascontiguousarray(np.asarray(a, dtype=np.float32))
    rel = f(inp['rel_bias'])
    idx = np.minimum(831 - np.arange(832), 512)
    xrb = np.ascontiguousarray(rel[:, :, idx])
    shared = dict(
        norm_g=f(inp['norm_g']), w_ada=f(inp['w_ada']), b_ada=f(inp['b_ada']), w_in=f(inp['w_in']), xrb=xrb,
        a_re=f(inp['ssm_a_re']), a_im=f(inp['ssm_a_im']), log_dt=f(inp['ssm_log_dt']),
        b_re=f(inp['ssm_b_re']), b_im=f(inp['ssm_b_im']), c_re=f(inp['ssm_c_re']), c_im=f(inp['ssm_c_im']),
        ssm_d=f(inp['ssm_d']).reshape(NL, 512), w_glu=f(inp['w_glu']), b_glu=f(inp['b_glu']),
        w_pool=f(inp['w_pool']), pool_scale=f(inp['pool_scale']), bng=f(inp['branch_norm_g']),
        w_out=f(inp['w_out']), fng=f(inp['final_norm_g']),
        identf=identf, j64=j64, invcnt=invcnt, masks=masks, bmask=bmask)
    maps = []
    for c in range(n_cores):
        sl = slice(4 * c, 4 * c + 4)
        m = dict(shared)
        m['xp'] = f(inp['x_prompt'][c, :NPT * TT])
        m['xs'] = f(inp['x_sample'][sl]).reshape(256, D)
        m['cvec'] = np.ascontiguousarray(np.concatenate([f(inp['c_prompt'][c:c + 1]), f(inp['c_sample'][sl])], 0))
        m['ck'] = f(inp['cache_k'][:, sl]).reshape(NL, 4, 512, 1024)
        m['cv'] = f(inp['cache_v'][:, sl]).reshape(NL, 4, 512, 1024)
        m['sre'] = f(inp['state_ssm_re'][:, sl])
        m['sim'] = f(inp['state_ssm_im'][:, sl])
        m['spool'] = f(inp['state_pool'][:, sl])
        maps.append(m)
    return maps


def run(inp, NPT=NPT_FULL, with_sample=True, nlayers=NL, n_cores=8):
    key = (NPT, with_sample, nlayers)
    if key not in _NC_CACHE:
        _NC_CACHE[key] = build_nc(NPT, with_sample, nlayers)
    nc = _NC_CACHE[key]
    maps = _in_maps(inp, NPT, n_cores)
    res = run_bass_kernel_spmd(nc, maps, core_ids=list(range(n_cores)))
    return res.results


def kernel(**inp):
    R = run(inp)
    st = lambda k: np.stack([np.asarray(r[k]) for r in R], 0)
    y_prompt = st('yp')
    y_sample = st('ys').reshape(32, 64, D)
    nkp = st('nkp').transpose(1, 0, 2, 3).reshape(NL, 8, 512, 16, 64)
    nvp = st('nvp').transpose(1, 0, 2, 3).reshape(NL, 8, 512, 16, 64)
    nsrp = st('nsrp').transpose(1, 0, 2, 3)
    nsip = st('nsip').transpose(1, 0, 2, 3)
    npp = st('npp').transpose(1, 0, 2, 3)
    nks = st('nks').reshape(8, NL, 4, 64, 16, 64).transpose(1, 0, 2, 3, 4, 5).reshape(NL, 32, 64, 16, 64)
    nvs = st('nvs').reshape(8, NL, 4, 64, 16, 64).transpose(1, 0, 2, 3, 4, 5).reshape(NL, 32, 64, 16, 64)
    nsrs = st('nsrs').transpose(1, 0, 2, 3, 4).reshape(NL, 32, 32, 64)
    nsis = st('nsis').transpose(1, 0, 2, 3, 4).reshape(NL, 32, 32, 64)
    nps = st('nps').transpose(1, 0, 2, 3, 4).reshape(NL, 32, 15, 512)
    outs = (y_prompt, y_sample, nkp, nvp, nsrp, nsip, npp, nks, nvs, nsrs, nsis, nps)
    return tuple(np.ascontiguousarray(o.astype(np.float32)) for o in outs)
```

```python
import contextlib
import numpy as np
import concourse.bass as bass
import concourse.mybir as mybir
from concourse.bass_utils import run_bass_kernel_spmd

F32 = mybir.dt.float32
BF16 = mybir.dt.bfloat16
AF = mybir.ActivationFunctionType
ALU = mybir.AluOpType

NL = 4
D = 2048
KT = 16
TT = 512
NPT_FULL = 8
EPS = 1e-6
PI = float(np.pi)
ENGS = ('pe', 'act', 'dve', 'pool', 'sp')
DMA_KEYS = []


class Prog:
    def __init__(self, nc, es):
        self.nc = nc
        self.es = es
        self.ops = {e: [] for e in ENGS}
        self.cnt = {e: 0 for e in ENGS}
        self.sem = {e: es.enter_context(nc.semaphore("s_" + e)) for e in ('pe', 'act', 'dve', 'pool')}
        self.dsem = {}
        self.dcum = {}
        self.last_w = {}
        self.readers = {}
        self.out_toks = []

    def dma_sem(self, key):
        if key not in self.dsem:
            self.dsem[key] = [self.es.enter_context(self.nc.semaphore("d_%d" % len(self.dsem))), 0]
            self.dcum[id(self.dsem[key][0])] = self.dsem[key]
            DMA_KEYS.append(key)
        return self.dsem[key]

    def op(self, eng, fn, reads=(), writes=(), dkey=None, out=False):
        deps = {}

        def add(tok):
            if tok is None:
                return
            s, v = tok
            cum = self.dcum.get(id(s))
            if cum is not None:
                v = cum[1]
            if deps.get(id(s), (None, 0))[1] < v:
                deps[id(s)] = (s, v)

        for k in reads:
            add(self.last_w.get(k))
        for k in writes:
            add(self.last_w.get(k))
            for t in self.readers.get(k, {}).values():
                add(t)
        if dkey is None:
            self.cnt[eng] += 1
            tok = (self.sem[eng], self.cnt[eng])
            inc = (self.sem[eng], 1)
        else:
            d = self.dma_sem(dkey)
            d[1] += 16
            tok = (d[0], d[1])
            inc = (d[0], 16)
        self.ops[eng].append((list(deps.values()), fn, inc))
        for k in reads:
            r = self.readers.setdefault(k, {})
            if r.get(id(tok[0]), (None, 0))[1] < tok[1]:
                r[id(tok[0])] = tok
        for k in writes:
            self.last_w[k] = tok
            self.readers[k] = {}
        if out:
            self.out_toks.append(tok)
        return tok

    def barrier(self):
        toks = [(self.sem[e], self.cnt[e]) for e in ('pe', 'act', 'dve', 'pool') if self.cnt[e] > 0]
        toks += [(d[0], d[1]) for d in self.dsem.values() if d[1] > 0]
        for e in ENGS:
            self.ops[e].append((list(toks), None, None))

    def finish(self):
        toks = {}
        for s, v in self.out_toks:
            if toks.get(id(s), (None, 0))[1] < v:
                toks[id(s)] = (s, v)
        self.ops['sp'].append((list(toks.values()), None, None))

    def emit(self):
        nc = self.nc
        sem_pe = self.sem['pe']
        ops = self.ops

        def run(eng, e):
            seen = {}
            for deps, fn, inc in ops[eng]:
                for s, v in deps:
                    if eng == 'pe' and s is sem_pe:
                        continue
                    if seen.get(id(s), 0) >= v:
                        continue
                    seen[id(s)] = v
                    e.wait_ge(s, v)
                if fn is not None:
                    ins = fn(e)
                    ins.then_inc(inc[0], inc[1])

        with nc.Block() as block:
            @block.tensor
            def _(e):
                run('pe', e)

            @block.scalar
            def _(e):
                run('act', e)

            @block.vector
            def _(e):
                run('dve', e)

            @block.gpsimd
            def _(e):
                run('pool', e)

            @block.sync
            def _(e):
                run('sp', e)


DEBUG = [False]


class _Stop(Exception):
    pass


def _chk(name):
    if STOP[0] == name:
        raise _Stop()

STOP = [None]


def build_nc(NPT=NPT_FULL, with_sample=True, nlayers=NL):
    nc = bass.Bass("TRN2", target_bir_lowering=False)
    es = contextlib.ExitStack()
    with es:
        _build(nc, es, NPT, with_sample, nlayers)
    return nc


def _build(nc, es, NPT, with_sample, NLR):
    P = Prog(nc, es)
    NTP = NPT * TT

    def din(name, shape, dt=F32):
        return nc.dram_tensor(name, list(shape), dt, kind="ExternalInput").ap()

    def dout(name, shape):
        return nc.dram_tensor(name, list(shape), F32, kind="ExternalOutput").ap()

    def dscr(name, shape, dt):
        return nc.dram_tensor(name, list(shape), dt).ap()

    def sb(name, shape, dt=F32):
        return es.enter_context(nc.sbuf_tensor(name, list(shape), dt))

    def ps(name, shape, dt=F32):
        return es.enter_context(nc.psum_tensor(name, list(shape), dt))

    xp = din("xp", [NTP, D]); xs = din("xs", [256, D]); cvec = din("cvec", [5, D])
    ck = din("ck", [NL, 4, 512, 1024]); cv = din("cv", [NL, 4, 512, 1024])
    sre = din("sre", [NL, 4, 32, 64]); sim = din("sim", [NL, 4, 32, 64]); spool = din("spool", [NL, 4, 15, 512])
    norm_g = din("norm_g", [NL, D]); w_ada = din("w_ada", [NL, D, 3 * D]); b_ada = din("b_ada", [NL, 3 * D])
    w_in = din("w_in", [NL, D, 6144]); xrb = din("xrb", [NL, 16, 832])
    a_re = din("a_re", [NL, 32, 64]); a_im = din("a_im", [NL, 32, 64]); log_dt = din("log_dt", [NL, 32])
    b_re = din("b_re", [NL, 32, 64, 16]); b_im = din("b_im", [NL, 32, 64, 16])
    c_re = din("c_re", [NL, 32, 16, 64]); c_im = din("c_im", [NL, 32, 16, 64])
    ssm_d = din("ssm_d", [NL, 512]); w_glu = din("w_glu", [NL, 512, 1024]); b_glu = din("b_glu", [NL, 1024])
    w_pool = din("w_pool", [NL, 4, 128, 128]); pool_scale = din("pool_scale", [NL, 512])
    bng = din("bng", [NL, D]); w_out = din("w_out", [NL, D, D]); fng = din("fng", [D])
    identf_d = din("identf", [128, 128]); j64_d = din("j64", [64, 64]); invcnt_d = din("invcnt", [128, 4, 16])
    masks_d = din("masks", [128, 2]); bmask_d = din("bmask", [128, 128])

    yp = dout("yp", [NTP, D]); ys = dout("ys", [256, D])
    nkp = dout("nkp", [NL, 512, 1024]); nvp = dout("nvp", [NL, 512, 1024])
    nsrp = dout("nsrp", [NL, 32, 64]); nsip = dout("nsip", [NL, 32, 64]); npp = dout("npp", [NL, 15, 512])
    nks = dout("nks", [NL, 256, 1024]); nvs = dout("nvs", [NL, 256, 1024])
    nsrs = dout("nsrs", [NL, 4, 32, 64]); nsis = dout("nsis", [NL, 4, 32, 64]); nps = dout("nps", [NL, 4, 15, 512])

    eb_d = dscr("eb_d", [NL, 2, 128, 8 * 640], BF16)
    lam_d = dscr("lam_d", [NL, 128, 16 * 9 * 3], F32)
    wz_d = dscr("wz_d", [NL, 16, 128, 4, 2, 128], BF16)
    bd_d = dscr("bd_d", [NL, 16, 128, 4, 128], BF16)
    wy_d = dscr("wy_d", [NL, 16, 128, 16, 2, 32], BF16)
    kh_d = dscr("kh_d", [NL, 128, 8, 512], BF16)
    vh_d = dscr("vh_d", [NL, 128, 4, 1024], BF16)

    xT = sb("xT", [128, KT, TT])
    hy = sb("hy", [128, KT, TT], BF16)
    NWB = 3
    wblk = [sb("wblk%d" % i, [128, KT, 128], BF16) for i in range(NWB)]
    qT = sb("qT", [128, 8, TT], BF16)
    kT = sb("kT", [128, 8, 1024], BF16)
    vb = sb("vb", [128, 8, 1024], BF16)
    zT = sb("zT", [128, KT, TT], BF16)
    ub = sb("ub", [128, 4, TT], BF16)
    UPW = 528
    upT = sb("upT", [128, 4, UPW])
    pta = sb("pta", [128, UPW]); ptb = sb("ptb", [128, UPW])
    ebt = sb("ebt", [128, 8, 640], BF16)
    lam = sb("lam", [128, 16, 9, 3])
    wzb = sb("wzb", [128, 4096], BF16)
    bdb = sb("bdb", [128, 16, 128], BF16)
    Zr = sb("Zr", [128, 16, 32]); Zi = sb("Zi", [128, 16, 32])
    Sp = sb("Sp", [128, 2, 16, 32], BF16)
    bmask = sb("bmask_s", [128, 128])
    stR = sb("stR", [128, NL, 16, 4]); stI = sb("stI", [128, NL, 16, 4]); uph = sb("uph", [128, NL, 4, 15])
    gT = sb("gT", [128, 4, TT], BF16)
    wpl = sb("wpl", [128, 4, 128], BF16)
    identf = sb("identf_s", [128, 128]); identb = sb("identb", [128, 128], BF16)
    onesb = sb("onesb", [128, 128], BF16); j64 = sb("j64_s", [64, 64])
    masks = sb("masks_s", [128, 2]); invcnt = sb("invcnt_s", [128, 4, 16])
    modT = sb("modT", [128, NL, 48, 5])
    Gm = sb("Gm", [128, NL, KT, 5])
    ngT = sb("ngT", [128, NL, KT]); bngT = sb("bngT", [128, NL, KT]); fngT = sb("fngT", [128, KT])
    badaT = sb("badaT", [128, NL, 48]); pscT = sb("pscT", [128, NL, 4]); bgluT = sb("bgluT", [128, NL, 8])
    dskT = sb("dskT", [128, NL, 4])
    scT = sb("scT", [128, KT, 8], BF16)
    rstd = sb("rstd", [128, TT]); tmpf = [sb("tmpf%d" % i, [128, TT]) for i in range(2)]
    sqb = [sb("sqb%d" % i, [128, TT], BF16) for i in range(2)]
    ef = [sb("ef%d" % i, [128, 640]) for i in range(2)]
    pt16 = [sb("pt16_%d" % i, [128, 640], BF16) for i in range(2)]
    rden = [sb("rden%d" % i, [128, 128]) for i in range(2)]
    stg = sb("stg", [128, D])
    kvs = [sb("kvs%d" % i, [128, 128]) for i in range(4)]
    ckb = [sb("ckb%d" % i, [128, 1024], BF16) for i in range(1)]

    psA = ps("psA", [128, 1024])
    psM = [ps("psM%d" % i, [128, 512]) for i in range(3)]
    psO = ps("psO", [128, 512])
    psT = ps("psT", [128, 512])
    psTb = ps("psTb", [128, 1024], BF16)

    mmi = [0]

    def next_psM():
        mmi[0] = (mmi[0] + 1) % 3
        return mmi[0]

    cnt = {}

    def rot(name, n):
        cnt[name] = (cnt.get(name, -1) + 1) % n
        return cnt[name]

    def dma(eng, out_ap, in_ap, key, reads=(), writes=(), out=False, nonc=False):
        def fn(e):
            if nonc:
                with nc.allow_non_contiguous_dma(reason="small strided parameter/state transfers"):
                    return e.dma_start(out=out_ap, in_=in_ap)
            return e.dma_start(out=out_ap, in_=in_ap)
        return P.op(eng, fn, reads=reads, writes=writes, dkey=key, out=out)

    def mm_group(out_ap, pairs, reads, writes, **kw):
        n = len(pairs)

        def fn(e):
            ins = None
            for i, (l_, r_) in enumerate(pairs):
                ins = e.matmul(out_ap, l_, r_, start=(i == 0), stop=(i == n - 1), **kw)
            return ins
        return P.op('pe', fn, reads=reads, writes=writes)

    dma('sp', identf[:], identf_d, 'c_ident', writes=['identf'])
    dma('sp', j64[:], j64_d, 'c_j64', writes=['j64'])
    dma('sp', masks[:], masks_d, 'c_masks', writes=['masks'])
    dma('sp', bmask[:], bmask_d, 'c_bmask', writes=['bmask'])
    dma('sp', invcnt[:], invcnt_d, 'c_invcnt', writes=['invcnt'])
    dma('pool', identb[:], identf_d, 'c_identb', writes=['identb'])
    P.op('dve', lambda e: e.memset(onesb[:], 1.0), writes=['onesb'])
    P.op('dve', lambda e: e.memset(kT[:], 0.0), writes=['kTh', 'kTn'])
    P.op('dve', lambda e: e.memset(vb[:], 0.0), writes=['vbh', 'vbn'])
    P.op('dve', lambda e: e.memset(upT[:], 0.0), writes=['upT'])
    P.op('dve', lambda e: e.memset(pta[:], 0.0), writes=['pta'])
    P.op('dve', lambda e: e.memset(ptb[:], 0.0), writes=['ptb'])
    P.op('dve', lambda e: e.memset(stR[:], 0.0), writes=['stR'])
    P.op('dve', lambda e: e.memset(stI[:], 0.0), writes=['stI'])
    P.op('dve', lambda e: e.memset(uph[:], 0.0), writes=['uph'])

    def fm_load(dst, src_row, key):
        dma('sp', dst, src_row.rearrange("(t p) -> p t", p=128), key, writes=[key], nonc=True)

    for l in range(NL):
        fm_load(ngT[:, l, :], norm_g[l], 'ngT%d' % l)
        fm_load(bngT[:, l, :], bng[l], 'bngT%d' % l)
        fm_load(badaT[:, l, :], b_ada[l], 'badaT%d' % l)
        fm_load(pscT[:, l, :], pool_scale[l], 'pscT%d' % l)
        fm_load(bgluT[:, l, :], b_glu[l], 'bgluT%d' % l)
        fm_load(dskT[:, l, :], ssm_d[l], 'dskT%d' % l)
    fm_load(fngT[:, :], fng, 'fngT')

    if STOP[0] == 's1':
        P.barrier(); P.finish(); P.emit(); return
    dma('sp', stg[0:5, :], cvec, 'stg_ld', writes=['stg'])
    P.op('act', lambda e: e.activation(out=stg[0:5, :], in_=stg[0:5, :], func=AF.Silu), reads=['stg'], writes=['stg'])

    def fn_ct(e):
        ins = None
        for kt in range(KT):
            ins = e.transpose(psT[:, kt * 8:kt * 8 + 5], stg[0:5, kt * 128:(kt + 1) * 128], identf[0:5, 0:5])
        return ins
    P.op('pe', fn_ct, reads=['stg', 'identf'], writes=['psT'])
    P.op('dve', lambda e: e.tensor_copy(out=scT[:, :, 0:5],
                                        in_=psT[:, 0:128].rearrange("p (k r) -> p k r", r=8)[:, :, 0:5]),
         reads=['psT'], writes=['scT'])

    wq = []
    wstate = {'issued': 0, 'used': 0}

    def w_issue_upto(n):
        while wstate['issued'] < min(n, len(wq)):
            i = wstate['issued']
            b = i % NWB
            kind_, src = wq[i]
            if kind_ == 'glu':
                wsrc, c4_ = src
                dma('pool', wblk[b][:, 0:4, :], wsrc[:, c4_ * 128:(c4_ + 1) * 128].rearrange("(k p) c -> p k c", p=128),
                    'wblk%d' % b, writes=['wblk%d' % b])
                dma('pool', wblk[b][:, 4:8, :],
                    wsrc[:, 512 + c4_ * 128:512 + (c4_ + 1) * 128].rearrange("(k p) c -> p k c", p=128),
                    'wblk%d' % b, writes=['wblk%d' % b])
            else:
                dma('pool', wblk[b][:, :, :], src.rearrange("(k p) c -> p k c", p=128), 'wblk%d' % b,
                    writes=['wblk%d' % b])
            wstate['issued'] += 1

    def w_next():
        i = wstate['used']
        w_issue_upto(i + NWB)
        wstate['used'] += 1
        return i % NWB

    IN_ORDER = list(range(8, 16)) + list(range(16, 24)) + list(range(32, 36)) + list(range(40, 44)) + \
        list(range(24, 32)) + list(range(36, 40)) + list(range(44, 48)) + list(range(0, 8))
    tiles = [('p', t) for t in range(NPT)] + ([('s', 0)] if with_sample else [])
    for l in range(NL):
        for cb in range(48):
            wq.append(('w', w_ada[l][:, cb * 128:(cb + 1) * 128]))
    for _tile in tiles:
        for l in range(NLR):
            for cb in IN_ORDER:
                wq.append(('w', w_in[l][:, cb * 128:(cb + 1) * 128]))
            for c4_ in range(4):
                wq.append(('glu', (w_glu[l], c4_)))
            for cb in range(16):
                wq.append(('w', w_out[l][:, cb * 128:(cb + 1) * 128]))

    for l in range(NL):
        for cb in range(48):
            b = w_next()
            m = next_psM()
            mm_group(psM[m][:, 0:5], [(wblk[b][:, kt, :], scT[:, kt, 0:5]) for kt in range(KT)],
                     reads=['wblk%d' % b, 'scT'], writes=['psM%d' % m])
            P.op('dve', (lambda l, cb, m: lambda e: e.tensor_scalar(
                out=modT[:, l, cb, :], in0=psM[m][:, 0:5], scalar1=badaT[:, l, cb:cb + 1], scalar2=None,
                op0=ALU.add))(l, cb, m),
                reads=['psM%d' % m, 'badaT%d' % l], writes=['modT'])
    for l in range(NL):
        P.op('dve', (lambda l: lambda e: e.tensor_scalar(out=Gm[:, l, :, :], in0=modT[:, l, 16:32, :], scalar1=1.0,
                                                         scalar2=None, op0=ALU.add))(l),
             reads=['modT'], writes=['Gm'])
        P.op('dve', (lambda l: lambda e: e.tensor_tensor(
            out=Gm[:, l, :, :], in0=Gm[:, l, :, :],
            in1=ngT[:, l, :].unsqueeze(2).broadcast_to([128, KT, 5]), op=ALU.mult))(l),
            reads=['Gm', 'ngT%d' % l], writes=['Gm'])

    if STOP[0] == 's2':
        P.barrier(); P.finish(); P.emit(); return
    tpair = stg[0:64, 0:2 * 704].rearrange("p (h r) -> p h r", h=2)
    for l in range(NLR):
        for par in range(2):
            for j in range(8):
                P.op('dve', lambda e: e.memset(stg[0:64, 0:1408], 0.0), writes=['stg'])
                src = bass.AP(xrb.tensor, (l * 16 + 2 * j) * 832, [[1, 64], [832, 2], [1, 576]])
                dma('sp', tpair[:, :, 64:640], src, 'stg_ld', writes=['stg'])
                P.op('act', lambda e: e.activation(out=tpair[:, :, 64:640], in_=tpair[:, :, 64:640], func=AF.Exp),
                     reads=['stg'], writes=['stg'])

                def fn(e, par=par):
                    ins = None
                    for p in range(5):
                        st = (64 + 128 * p) if par == 0 else 128 * p
                        for hh in range(2):
                            ins = e.matmul(psA[:, p * 128 + hh * 64:p * 128 + hh * 64 + 64],
                                           tpair[:, hh, st:st + 128], j64[:, :], start=True, stop=True)
                    return ins
                P.op('pe', fn, reads=['stg', 'j64'], writes=['psA'])
                P.op('dve', (lambda j: lambda e: e.tensor_copy(out=ebt[:, j, :], in_=psA[:, 0:640]))(j),
                     reads=['psA'], writes=['ebt'])
            dma('sp', eb_d[l, par], ebt[:].rearrange("p a c -> p (a c)"), 'eb_st', reads=['ebt'],
                writes=['eb_d%d_%d' % (l, par)])

    if STOP[0] == 's3':
        P.barrier(); P.finish(); P.emit(); return
    o = [0]

    def carve(n):
        a = stg[:, o[0]:o[0] + n]
        o[0] += n
        return a
    ar = carve(16); ai = carve(16); ldt = carve(16); dtv = carve(16); mag = carve(16); ang = carve(16)
    kacc = carve(16); rr = carve(16); sn = carve(16); cs_ = carve(16); t1 = carve(16); t2 = carve(16)
    cr_ = carve(16); ci_ = carve(16); den_ = carve(16)
    Br = carve(256); Bi = carve(256); Bbr = carve(256); Bbi = carve(256)
    Tm = carve(512)
    tB = Tm[:, 0:256]
    C2 = carve(128)
    Cr = Br; Ci = Bi
    assert o[0] <= 2048
    KS = 'stg'

    def V(fn_, extra_r=()):
        P.op('dve', fn_, reads=[KS] + list(extra_r), writes=[KS])

    def B3(a):
        return a.rearrange("p (a c) -> p a c", c=16)

    def bc16(a):
        return a.unsqueeze(2).broadcast_to([128, 16, 16])

    for l in range(NLR):
        for e_ in range(2):
            dma('sp', ar[64 * e_:64 * e_ + 64, :], bass.AP(a_re.tensor, (l * 32 + e_) * 64, [[1, 64], [128, 16]]),
                'stg_ld', writes=[KS], nonc=True)
            dma('sp', ai[64 * e_:64 * e_ + 64, :], bass.AP(a_im.tensor, (l * 32 + e_) * 64, [[1, 64], [128, 16]]),
                'stg_ld', writes=[KS], nonc=True)
            dma('sp', ldt[64 * e_:64 * e_ + 64, :], bass.AP(log_dt.tensor, l * 32 + e_, [[0, 64], [2, 16]]),
                'stg_ld', writes=[KS], nonc=True)
            dma('sp', B3(Br[64 * e_:64 * e_ + 64, :]),
                bass.AP(b_re.tensor, (l * 32 + e_) * 1024, [[16, 64], [2048, 16], [1, 16]]),
                'stg_ld', writes=[KS], nonc=True)
            dma('sp', B3(Bi[64 * e_:64 * e_ + 64, :]),
                bass.AP(b_im.tensor, (l * 32 + e_) * 1024, [[16, 64], [2048, 16], [1, 16]]),
                'stg_ld', writes=[KS], nonc=True)
        P.op('act', lambda e: e.activation(out=dtv, in_=ldt, func=AF.Exp), reads=[KS], writes=[KS])
        V(lambda e: e.tensor_tensor(out=mag, in0=ar, in1=dtv, op=ALU.mult))
        P.op('act', lambda e: e.activation(out=mag, in_=mag, func=AF.Exp), reads=[KS], writes=[KS])
        V(lambda e: e.tensor_tensor(out=ang, in0=ai, in1=dtv, op=ALU.mult))
        for shift, dst in ((0.0, sn), (PI / 2, cs_)):
            V((lambda shift: lambda e: e.tensor_scalar(out=rr, in0=ang, scalar1=shift, scalar2=None, op0=ALU.add))(shift))
            V(lambda e: e.memset(kacc, 0.0))
            for m_ in range(1, 9):
                V((lambda m_: lambda e: e.scalar_tensor_tensor(out=kacc, in0=rr, scalar=(2 * m_ - 1) * PI, in1=kacc,
                                                               op0=ALU.is_ge, op1=ALU.add))(m_))
            V(lambda e: e.scalar_tensor_tensor(out=rr, in0=kacc, scalar=-2 * PI, in1=rr, op0=ALU.mult, op1=ALU.add))
            P.op('act', (lambda dst: lambda e: e.activation(out=dst, in_=rr, func=AF.Sin))(dst),
                 reads=[KS], writes=[KS])
        V(lambda e: e.tensor_tensor(out=lam[:, :, 0, 0], in0=mag, in1=cs_, op=ALU.mult))
        V(lambda e: e.tensor_tensor(out=lam[:, :, 0, 1], in0=mag, in1=sn, op=ALU.mult))
        for k in range(1, 9):
            V((lambda k: lambda e: e.tensor_tensor(out=t1, in0=lam[:, :, k - 1, 0], in1=lam[:, :, k - 1, 0], op=ALU.mult))(k))
            V((lambda k: lambda e: e.tensor_tensor(out=t2, in0=lam[:, :, k - 1, 1], in1=lam[:, :, k - 1, 1], op=ALU.mult))(k))
            V((lambda k: lambda e: e.tensor_tensor(out=lam[:, :, k, 0], in0=t1, in1=t2, op=ALU.subtract))(k))
            V((lambda k: lambda e: e.tensor_tensor(out=t1, in0=lam[:, :, k - 1, 0], in1=lam[:, :, k - 1, 1], op=ALU.mult))(k))
            V((lambda k: lambda e: e.tensor_scalar(out=lam[:, :, k, 1], in0=t1, scalar1=2.0, scalar2=None, op0=ALU.mult))(k))
        V(lambda e: e.tensor_scalar(out=lam[:, :, :, 2], in0=lam[:, :, :, 1], scalar1=-1.0, scalar2=None, op0=ALU.mult))
        dma('sp', lam_d[l], lam[:].rearrange("p a b c -> p (a b c)"), 'lam_st', reads=[KS], writes=['lam_d%d' % l])
        V(lambda e: e.tensor_tensor(out=den_, in0=ar, in1=ar, op=ALU.mult))
        V(lambda e: e.tensor_tensor(out=t1, in0=ai, in1=ai, op=ALU.mult))
        V(lambda e: e.tensor_tensor(out=den_, in0=den_, in1=t1, op=ALU.add))
        V(lambda e: e.reciprocal(out=den_, in_=den_))
        V(lambda e: e.tensor_scalar(out=t1, in0=lam[:, :, 0, 0], scalar1=-1.0, scalar2=None, op0=ALU.add))
        V(lambda e: e.tensor_tensor(out=cr_, in0=t1, in1=ar, op=ALU.mult))
        V(lambda e: e.tensor_tensor(out=t2, in0=lam[:, :, 0, 1], in1=ai, op=ALU.mult))
        V(lambda e: e.tensor_tensor(out=cr_, in0=cr_, in1=t2, op=ALU.add))
        V(lambda e: e.tensor_tensor(out=cr_, in0=cr_, in1=den_, op=ALU.mult))
        V(lambda e: e.tensor_tensor(out=ci_, in0=lam[:, :, 0, 1], in1=ar, op=ALU.mult))
        V(lambda e: e.tensor_tensor(out=t2, in0=t1, in1=ai, op=ALU.mult))
        V(lambda e: e.tensor_tensor(out=ci_, in0=ci_, in1=t2, op=ALU.subtract))
        V(lambda e: e.tensor_tensor(out=ci_, in0=ci_, in1=den_, op=ALU.mult))
        V(lambda e: e.tensor_tensor(out=B3(Bbr), in0=B3(Br), in1=bc16(cr_), op=ALU.mult))
        V(lambda e: e.tensor_tensor(out=B3(tB), in0=B3(Bi), in1=bc16(ci_), op=ALU.mult))
        V(lambda e: e.tensor_tensor(out=Bbr, in0=Bbr, in1=tB, op=ALU.subtract))
        V(lambda e: e.tensor_tensor(out=B3(Bbi), in0=B3(Bi), in1=bc16(cr_), op=ALU.mult))
        V(lambda e: e.tensor_tensor(out=B3(tB), in0=B3(Br), in1=bc16(ci_), op=ALU.mult))
        V(lambda e: e.tensor_tensor(out=Bbi, in0=Bbi, in1=tB, op=ALU.add))
        for csrc, cdst in ((c_re, Cr), (c_im, Ci)):
            for blk in range(2):
                for p8 in range(8):
                    g0 = 2 * (8 * blk + p8)
                    dma('sp', C2[16 * p8:16 * p8 + 16, :].rearrange("c (e p) -> c e p", e=2),
                        bass.AP(csrc.tensor, (l * 32 + g0) * 1024, [[64, 16], [1024, 2], [1, 64]]),
                        'stg_ld', writes=[KS])
                P.op('pe', lambda e: e.transpose(psT[:, 0:128], C2, identf[:, :]), reads=[KS, 'identf'],
                     writes=['psT'])
                P.op('dve', (lambda cdst, blk: lambda e: e.tensor_copy(out=cdst[:, blk * 128:(blk + 1) * 128],
                                                                       in_=psT[:, 0:128]))(cdst, blk),
                     reads=['psT'], writes=[KS])
        arena = xT[:].rearrange("p k c -> p (k c)")
        ao = [0]

        def acarve(n):
            a_ = arena[:, ao[0]:ao[0] + n]
            ao[0] += n
            return a_
        zr = acarve(256); zi = acarve(256); TmR = acarve(512); TmI = acarve(512); CMr = acarve(512); CMi = acarve(512)
        cur_r = acarve(16); cur_i = acarve(16); nr_ = acarve(16); ni_ = acarve(16)
        yr = acarve(256); yi = acarve(256); ta = acarve(256)
        KX = 'xT'

        def VX(fn_, r=(), w=()):
            P.op('dve', fn_, reads=[KS, KX, 'masks'] + list(r), writes=[KX] + list(w))
        T4 = lambda a_: a_.rearrange("p (a e c) -> p a e c", e=2, c=16)
        for e_ in range(2):
            VX((lambda e_: lambda e: e.tensor_scalar(out=T4(CMr)[:, :, e_, :], in0=B3(Cr), scalar1=masks[:, e_:e_ + 1],
                                                     scalar2=None, op0=ALU.mult))(e_))
            VX((lambda e_: lambda e: e.tensor_scalar(out=T4(CMi)[:, :, e_, :], in0=B3(Ci), scalar1=masks[:, e_:e_ + 1],
                                                     scalar2=-1.0, op0=ALU.mult, op1=ALU.mult))(e_))
        VX(lambda e: e.memset(cur_r, 1.0))
        VX(lambda e: e.memset(cur_i, 0.0))
        wzs = wzb[:, 0:1024].rearrange("p (j r c) -> p j r c", r=2, c=128)
        wys = wzb[:, 1024:2048].rearrange("p (a r c) -> p a r c", r=2, c=32)
        for tau in range(17):
            if tau <= 15:
                VX(lambda e: e.tensor_tensor(out=B3(zr), in0=B3(Bbr), in1=bc16(cur_r), op=ALU.mult))
                VX(lambda e: e.tensor_tensor(out=B3(ta), in0=B3(Bbi), in1=bc16(cur_i), op=ALU.mult))
                VX(lambda e: e.tensor_tensor(out=zr, in0=zr, in1=ta, op=ALU.subtract))
                VX(lambda e: e.tensor_tensor(out=B3(zi), in0=B3(Bbi), in1=bc16(cur_r), op=ALU.mult))
                VX(lambda e: e.tensor_tensor(out=B3(ta), in0=B3(Bbr), in1=bc16(cur_i), op=ALU.mult))
                VX(lambda e: e.tensor_tensor(out=zi, in0=zi, in1=ta, op=ALU.add))
                for e_ in range(2):
                    VX((lambda e_: lambda e: e.tensor_scalar(out=T4(TmR)[:, :, e_, :], in0=B3(zr), scalar1=masks[:, e_:e_ + 1],
                                                             scalar2=None, op0=ALU.mult))(e_))
                    VX((lambda e_: lambda e: e.tensor_scalar(out=T4(TmI)[:, :, e_, :], in0=B3(zi), scalar1=masks[:, e_:e_ + 1],
                                                             scalar2=None, op0=ALU.mult))(e_))
                for ri, TmX in ((0, TmR), (1, TmI)):
                    def fn_t(e, TmX=TmX):
                        ins = None
                        for j in range(4):
                            ins = e.transpose(psT[:, j * 128:(j + 1) * 128], TmX[:, j * 128:(j + 1) * 128], identf[:, :])
                        return ins
                    P.op('pe', fn_t, reads=[KX, 'identf'], writes=['psT'])
                    P.op('dve', (lambda ri: lambda e: e.tensor_copy(out=wzs[:, :, ri, :],
                                                                    in_=psT[:, :].rearrange("p (j c) -> p j c", c=128)))(ri),
                         reads=['psT'], writes=['wzb'])
                dma('sp', wz_d[l, 15 - tau].rearrange("p j r c -> p (j r c)"), wzb[:, 0:1024], 'wz_st', reads=['wzb'],
                    writes=['wz_d%d' % l])
                for j in range(4):
                    m = next_psM()

                    def fn_bd(e, j=j, m=m):
                        e.matmul(psM[m][:, 0:128], TmR[:, j * 128:(j + 1) * 128], CMr[:, j * 128:(j + 1) * 128],
                                 start=True, stop=False)
                        return e.matmul(psM[m][:, 0:128], TmI[:, j * 128:(j + 1) * 128], CMi[:, j * 128:(j + 1) * 128],
                                        start=False, stop=True)
                    P.op('pe', fn_bd, reads=[KX], writes=['psM%d' % m])
                    P.op('dve', (lambda j, m: lambda e: e.tensor_tensor(out=bdb[:, j, :], in0=psM[m][:, 0:128],
                                                                        in1=bmask[:, :], op=ALU.mult))(j, m),
                         reads=['psM%d' % m, 'bmask'], writes=['bdb'])
                dma('sp', bd_d[l, tau].rearrange("p j c -> p (j c)"), bdb[:, 0:4, :].rearrange("p j c -> p (j c)"), 'bd_st',
                    reads=['bdb'], writes=['bd_d%d' % l])
            if tau >= 1:
                VX(lambda e: e.tensor_tensor(out=B3(yr), in0=B3(Cr), in1=bc16(cur_r), op=ALU.mult))
                VX(lambda e: e.tensor_tensor(out=B3(ta), in0=B3(Ci), in1=bc16(cur_i), op=ALU.mult))
                VX(lambda e: e.tensor_tensor(out=yr, in0=yr, in1=ta, op=ALU.subtract))
                VX(lambda e: e.tensor_tensor(out=B3(yi), in0=B3(Cr), in1=bc16(cur_i), op=ALU.mult))
                VX(lambda e: e.tensor_tensor(out=B3(ta), in0=B3(Ci), in1=bc16(cur_r), op=ALU.mult))
                VX(lambda e: e.tensor_tensor(out=yi, in0=yi, in1=ta, op=ALU.add))
                P.op('dve', lambda e: e.memset(wzb[:, 1024:2048], 0.0), writes=['wzb'])
                for e_ in range(2):
                    P.op('dve', (lambda e_: lambda e: e.tensor_scalar(
                        out=wys[:, :, 0, 16 * e_:16 * e_ + 16], in0=B3(yr), scalar1=masks[:, e_:e_ + 1], scalar2=None,
                        op0=ALU.mult))(e_), reads=[KX, 'masks'], writes=['wzb'])
                    P.op('dve', (lambda e_: lambda e: e.tensor_scalar(
                        out=wys[:, :, 1, 16 * e_:16 * e_ + 16], in0=B3(yi), scalar1=masks[:, e_:e_ + 1], scalar2=-1.0,
                        op0=ALU.mult, op1=ALU.mult))(e_), reads=[KX, 'masks'], writes=['wzb'])
                dma('sp', wy_d[l, tau - 1].rearrange("p a r c -> p (a r c)"), wzb[:, 1024:2048], 'wy_st', reads=['wzb'],
                    writes=['wy_d%d' % l])
            VX(lambda e: e.tensor_tensor(out=nr_, in0=cur_r, in1=lam[:, :, 0, 0], op=ALU.mult), r=['lamp'])
            VX(lambda e: e.tensor_tensor(out=ta[:, 0:16], in0=cur_i, in1=lam[:, :, 0, 1], op=ALU.mult))
            VX(lambda e: e.tensor_tensor(out=nr_, in0=nr_, in1=ta[:, 0:16], op=ALU.subtract))
            VX(lambda e: e.tensor_tensor(out=ni_, in0=cur_r, in1=lam[:, :, 0, 1], op=ALU.mult))
            VX(lambda e: e.tensor_tensor(out=ta[:, 0:16], in0=cur_i, in1=lam[:, :, 0, 0], op=ALU.mult))
            VX(lambda e: e.tensor_tensor(out=ni_, in0=ni_, in1=ta[:, 0:16], op=ALU.add))
            VX(lambda e: e.tensor_copy(out=cur_r, in_=nr_))
            VX(lambda e: e.tensor_copy(out=cur_i, in_=ni_))

    P.barrier()
    if STOP[0] == 's4':
        P.finish(); P.emit(); return

    cur_half = [0]

    def do_tile(kind, t):
        is_p = (kind == 'p')
        NTOK = TT if is_p else 256
        NB = NTOK // 128
        NSEG = 1 if is_p else 4
        SEGL = NTOK // NSEG
        NCH = NTOK // 64
        PAD = 256 if is_p else 32
        SW = PAD + SEGL
        NSTEP = 9 if is_p else 6
        rows = [0] if is_p else [1, 2, 3, 4]
        last_p = is_p and (t == NPT - 1)
        need_out = last_p or (not is_p)
        if not is_p:
            P.op('dve', lambda e: e.memset(upT[:], 0.0), writes=['upT'])

        def seg3(ap2):
            return ap2[:, 0:NSEG * SW].rearrange("p (s w) -> p s w", w=SW)

        def tok3(ap2):
            return ap2.rearrange("p (s w) -> p s w", w=SEGL)

        xsrc = xp[t * TT:(t + 1) * TT, :] if is_p else xs
        for tb in range(NB):
            dma('sp', stg[:], xsrc[tb * 128:(tb + 1) * 128, :], 'stg_ld', writes=['stg'])
            for k4 in range(4):
                def fn(e, k4=k4):
                    ins = None
                    for kk in range(4):
                        kt = k4 * 4 + kk
                        ins = e.transpose(psT[:, kk * 128:(kk + 1) * 128], stg[:, kt * 128:(kt + 1) * 128], identf[:, :])
                    return ins
                P.op('pe', fn, reads=['stg', 'identf'], writes=['psT'])
                P.op('dve', (lambda tb, k4: lambda e: e.tensor_copy(
                    out=xT[:, k4 * 4:(k4 + 1) * 4, tb * 128:(tb + 1) * 128],
                    in_=psT[:, :].rearrange("p (k c) -> p k c", c=128)))(tb, k4),
                    reads=['psT'], writes=['xT'])

        def make_rstd(m, width):
            P.op('dve', lambda e: e.tensor_scalar(out=rstd[:, 0:NTOK], in0=psM[m][:, 0:NTOK], scalar1=1.0 / width,
                                                  scalar2=EPS, op0=ALU.mult, op1=ALU.add),
                 reads=['psM%d' % m], writes=['rstd'])
            P.op('act', lambda e: e.activation(out=rstd[:, 0:NTOK], in_=rstd[:, 0:NTOK], func=AF.Sqrt),
                 reads=['rstd'], writes=['rstd'])
            P.op('dve', lambda e: e.reciprocal(out=rstd[:, 0:NTOK], in_=rstd[:, 0:NTOK]), reads=['rstd'],
                 writes=['rstd'])
        _chk('m1')

        def do_layer(l):
            cur = 1
            prev = 0
            dma('sp', lam[:].rearrange("p a b c -> p (a b c)"), lam_d[l], 'lam_ld', reads=['lam_d%d' % l], writes=['lam'])
            dma('pool', wpl[:], w_pool[l].rearrange("g c d -> c g d"), 'wpl_ld', writes=['wpl'])
            if is_p and t > 0:
                dma('sp', kT[:, :, 0:512], kh_d[l], 'kh_ld', reads=['kh_d%d' % l], writes=['kTh'])
                dma('sp', vb[:, 0:4, :], vh_d[l], 'vh_ld', reads=['vh_d%d' % l], writes=['vbh'])
                P.op('dve', lambda e: e.tensor_copy(out=upT[:, :, 0:15], in_=uph[:, l, :, :]), reads=['uph'], writes=['upT'])

            m = next_psM()
            for kt in range(KT):
                qi = rot('sqb', 2)
                P.op('act', (lambda kt, qi: lambda e: e.activation(out=sqb[qi][:, 0:NTOK], in_=xT[:, kt, 0:NTOK],
                                                                   func=AF.Square))(kt, qi),
                     reads=['xT'], writes=['sqb%d' % qi])
                P.op('pe', (lambda kt, qi, m: lambda e: e.matmul(psM[m][:, 0:NTOK], onesb[:, :], sqb[qi][:, 0:NTOK],
                                                                 start=(kt == 0), stop=(kt == KT - 1)))(kt, qi, m),
                     reads=['sqb%d' % qi, 'onesb'], writes=['psM%d' % m])

            make_rstd(m, D)
            for kt in range(KT):
                ti = rot('tmpf', 2)
                for sg in range(NSEG):
                    r = rows[sg]
                    c0, c1 = sg * SEGL, (sg + 1) * SEGL
                    P.op('dve', (lambda kt, ti, r, c0, c1: lambda e: e.scalar_tensor_tensor(
                        out=tmpf[ti][:, c0:c1], in0=xT[:, kt, c0:c1], scalar=Gm[:, l, kt, r:r + 1],
                        in1=rstd[:, c0:c1], op0=ALU.mult, op1=ALU.mult))(kt, ti, r, c0, c1),
                        reads=['xT', 'Gm', 'rstd'], writes=['tmpf%d' % ti])
                    P.op('act', (lambda kt, ti, r, c0, c1: lambda e: e.activation(
                        out=hy[:, kt, c0:c1], in_=tmpf[ti][:, c0:c1], func=AF.Identity,
                        bias=modT[:, l, kt, r:r + 1], scale=1.0))(kt, ti, r, c0, c1),
                        reads=['tmpf%d' % ti, 'modT'], writes=['hy'])

            _chk('m2')
            if not is_p:
                for sg in range(4):
                    dma('sp', stg[0:15, 0:512], spool[l, sg], 'stg_ld', writes=['stg'])

                    def fn(e):
                        ins = None
                        for g in range(4):
                            ins = e.transpose(psT[:, g * 16:g * 16 + 15], stg[0:15, g * 128:(g + 1) * 128],
                                              identf[0:15, 0:15])
                        return ins
                    P.op('pe', fn, reads=['stg', 'identf'], writes=['psT'])
                    P.op('dve', (lambda sg: lambda e: e.tensor_copy(
                        out=upT[:, :, sg * 79:sg * 79 + 15],
                        in_=psT[:, 0:64].rearrange("p (g c) -> p g c", c=16)[:, :, 0:15]))(sg),
                        reads=['psT'], writes=['upT'])
                    for e_ in range(2):
                        dma('sp', stR[64 * e_:64 * e_ + 64, l, :, sg],
                            bass.AP(sre.tensor, ((l * 4 + sg) * 32 + e_) * 64, [[1, 64], [128, 16]]),
                            'st_ld', writes=['stR'], nonc=True)
                        dma('sp', stI[64 * e_:64 * e_ + 64, l, :, sg],
                            bass.AP(sim.tensor, ((l * 4 + sg) * 32 + e_) * 64, [[1, 64], [128, 16]]),
                            'st_ld', writes=['stI'], nonc=True)

            def ext_cols(sg):
                if is_p:
                    return cur * 512, cur * 512 + 512
                return 512 + 128 * sg, 512 + 128 * sg + 64

            def proj_fm(b, m):
                mm_group(psM[m][:, 0:NTOK], [(wblk[b][:, kt, :], hy[:, kt, 0:NTOK]) for kt in range(KT)],
                         reads=['wblk%d' % b, 'hy'], writes=['psM%d' % m])

            def proj_tm(b, j, dst_bf_fn, odram):
                for tb in range(NB):
                    m = next_psM()
                    mm_group(psM[m][:, 0:128],
                             [(hy[:, kt, tb * 128:(tb + 1) * 128], wblk[b][:, kt, :]) for kt in range(KT)],
                             reads=['wblk%d' % b, 'hy'], writes=['psM%d' % m])
                    if dst_bf_fn is not None:
                        dst_bf_fn(tb, m)
                    if odram is not None:
                        ki = rot('kvs', 4)
                        P.op('act', (lambda m, ki: lambda e: e.copy(out=kvs[ki][:, :], in_=psM[m][:, 0:128]))(m, ki),
                             reads=['psM%d' % m], writes=['kvs%d' % ki])
                        dma('sp', odram[l, tb * 128:(tb + 1) * 128, j * 128:(j + 1) * 128], kvs[ki][:, :],
                            'kvs_o%d' % ki, reads=['kvs%d' % ki], writes=['o_kv'], out=True)

            for cb in IN_ORDER:
                b = w_next()
                if 8 <= cb < 16:
                    j = cb - 8
                    m = next_psM(); proj_fm(b, m)
                    for sg in range(NSEG):
                        c0, c1 = ext_cols(sg)
                        P.op('dve', (lambda j, m, sg, c0, c1: lambda e: e.tensor_copy(
                            out=kT[:, j, c0:c1], in_=psM[m][:, sg * SEGL:(sg + 1) * SEGL]))(j, m, sg, c0, c1),
                            reads=['psM%d' % m], writes=['kTn'])
                    if need_out:
                        proj_tm(b, j, None, nkp if is_p else nks)
                elif 16 <= cb < 24:
                    j = cb - 16

                    def vdst(tb, m, j=j):
                        if is_p:
                            P.op('dve', lambda e: e.tensor_copy(out=vb[:, cur * 4 + tb, j * 128:(j + 1) * 128],
                                                                in_=psM[m][:, 0:128]),
                                 reads=['psM%d' % m], writes=['vbn'])
                        else:
                            for hh in range(2):
                                sg = 2 * tb + hh
                                P.op('dve', (lambda sg, hh: lambda e: e.tensor_copy(
                                    out=vb[0:64, 4 + sg, j * 128:(j + 1) * 128],
                                    in_=psM[m][64 * hh:64 * hh + 64, 0:128]))(sg, hh),
                                    reads=['psM%d' % m], writes=['vbn'])
                    proj_tm(b, j, vdst, (nvp if is_p else nvs) if need_out else None)
                elif 32 <= cb < 36:
                    j = cb - 32
                    m = next_psM(); proj_fm(b, m)
                    P.op('act', (lambda j, m: lambda e: e.copy(out=ub[:, j, 0:NTOK], in_=psM[m][:, 0:NTOK]))(j, m),
                         reads=['psM%d' % m], writes=['ub'])
                elif 40 <= cb < 44:
                    g = cb - 40
                    m = next_psM(); proj_fm(b, m)
                    P.op('act', (lambda g, m: lambda e: e.copy(
                        out=upT[:, g, 0:NSEG * (15 + SEGL)].rearrange("p (s w) -> p s w", w=15 + SEGL)[:, :, 15:15 + SEGL],
                        in_=tok3(psM[m][:, 0:NTOK])))(g, m),
                        reads=['psM%d' % m], writes=['upT'])
                elif cb < 8:
                    j = cb
                    m = next_psM(); proj_fm(b, m)
                    P.op('dve', (lambda j, m: lambda e: e.tensor_copy(out=qT[:, j, 0:NTOK], in_=psM[m][:, 0:NTOK]))(j, m),
                         reads=['psM%d' % m], writes=['qT'])
                else:
                    if 24 <= cb < 32:
                        zi = cb - 24
                    elif 36 <= cb < 40:
                        zi = 8 + cb - 36
                    else:
                        zi = 12 + cb - 44
                    m = next_psM(); proj_fm(b, m)
                    P.op('act', (lambda zi, m: lambda e: e.activation(out=zT[:, zi, 0:NTOK], in_=psM[m][:, 0:NTOK],
                                                                      func=AF.Silu))(zi, m),
                         reads=['psM%d' % m], writes=['zT'])

            _chk('m3')
            if is_p and not last_p:
                dma('sp', kh_d[l], kT[:, :, 512:1024], 'kh_st', reads=['kTn'], writes=['kh_d%d' % l])
                dma('sp', vh_d[l], vb[:, 4:8, :], 'vh_st', reads=['vbn'], writes=['vh_d%d' % l])

            def band_pieces(mc):
                if not is_p:
                    return [(p, 0, 128, p) for p in range(4)] + [(4 + mc, 0, 64, 4)]
                out_ = []
                if mc % 2 == 0:
                    for p in range(5):
                        c0 = mc - 8 + 2 * p
                        if c0 < 0 and t == 0:
                            continue
                        if p == 4:
                            out_.append((cur * 4 + mc // 2, 0, 64, 4))
                        elif c0 < 0:
                            out_.append((prev * 4 + (c0 + 8) // 2, 0, 128, p))
                        else:
                            out_.append((cur * 4 + c0 // 2, 0, 128, p))
                else:
                    for p in range(5):
                        c1 = mc - 8 + 2 * p
                        c0 = c1 - 1
                        if p == 0:
                            if t == 0:
                                continue
                            out_.append((prev * 4 + (c1 + 8) // 2, 64, 128, 0))
                        elif c0 < 0:
                            if t == 0:
                                continue
                            out_.append((prev * 4 + (c0 + 8) // 2, 0, 128, p))
                        else:
                            out_.append((cur * 4 + c0 // 2, 0, 128, p))
                return out_

            def load_hist(sg):
                dma('pool', vb[:, 0:4, :], cv[l, sg].rearrange("(b p) f -> p b f", p=128), 'vb_ld', writes=['vbh'])
                for rb in range(4):
                    ci = rot("ckb", 1)
                    dma('pool', ckb[ci][:], ck[l, sg, rb * 128:(rb + 1) * 128, :], 'ckb_ld%d' % ci, writes=['ckb%d' % ci])

                    def fn(e, ci=ci):
                        ins = None
                        for ft in range(8):
                            ins = e.transpose(psTb[:, ft * 128:(ft + 1) * 128], ckb[ci][:, ft * 128:(ft + 1) * 128],
                                              identb[:, :])
                        return ins
                    P.op('pe', fn, reads=['ckb%d' % ci, 'identb'], writes=['psTb'])
                    P.op('dve', (lambda rb: lambda e: e.tensor_copy(
                        out=kT[:, :, rb * 128:(rb + 1) * 128],
                        in_=psTb[:, :].rearrange("p (f c) -> p f c", c=128)))(rb),
                        reads=['psTb'], writes=['kTh'])

            def attn_unit(j, mc, tokc0):
                pcs = band_pieces(mc)
                ei = rot('ef', 2)

                def fn_s(e):
                    ins = None
                    for (blk, lo, hi, pp) in pcs:
                        for hh in range(2):
                            ins = e.matmul(psA[:, 512 * hh + pp * 64:512 * hh + pp * 64 + 64],
                                           kT[64 * hh:64 * hh + 64, j, blk * 128:(blk + 1) * 128],
                                           qT[64 * hh:64 * hh + 64, j, tokc0:tokc0 + 64], start=True, stop=True,
                                           tile_position=(64 * hh, 0))
                    return ins
                P.op('pe', fn_s, reads=['kTh', 'kTn', 'qT'], writes=['psA'])
                p_lo = min(pp for (_, _, _, pp) in pcs)
                p_hi = max(pp for (_, _, _, pp) in pcs) + 1
                P.op('act', lambda e: e.activation(
                    out=ef[ei][:, p_lo * 128:p_hi * 128].rearrange("p (q h i) -> p q h i", h=2, i=64),
                    in_=psA[:, :].rearrange("p (h q i) -> p q h i", h=2, i=64)[:, p_lo:p_hi, :, :],
                    func=AF.Exp, scale=0.125),
                     reads=['psA'], writes=['ef%d' % ei])
                P.op('dve', lambda e: e.tensor_tensor(out=pt16[ei][:, p_lo * 128:p_hi * 128],
                                                      in0=ef[ei][:, p_lo * 128:p_hi * 128],
                                                      in1=ebt[:, j, p_lo * 128:p_hi * 128], op=ALU.mult),
                     reads=['ef%d' % ei, 'ebt'], writes=['pt16_%d' % ei])
                n = len(pcs)

                def fn_o(e):
                    ins = None
                    for dst0, use_v in ((0, True), (128, False)):
                        for i, (blk, lo, hi, pp) in enumerate(pcs):
                            lhs = vb[:, blk, j * 128:(j + 1) * 128] if use_v else onesb[:, :]
                            ins = e.matmul(psO[:, dst0:dst0 + 128], lhs, pt16[ei][:, pp * 128:(pp + 1) * 128],
                                           start=(i == 0), stop=(i == n - 1))
                    return ins
                P.op('pe', fn_o, reads=['vbh', 'vbn', 'pt16_%d' % ei, 'onesb'], writes=['psO'])
                P.op('dve', lambda e: e.reciprocal(out=rden[ei][:, :], in_=psO[:, 128:256]), reads=['psO'],
                     writes=['rden%d' % ei])
                for hh in range(2):
                    P.op('dve', (lambda hh: lambda e: e.tensor_tensor(
                        out=hy[64 * hh:64 * hh + 64, j, tokc0:tokc0 + 64],
                        in0=psO[64 * hh:64 * hh + 64, 64 * hh:64 * hh + 64],
                        in1=rden[ei][64 * hh:64 * hh + 64, 64 * hh:64 * hh + 64], op=ALU.mult))(hh),
                        reads=['psO', 'rden%d' % ei], writes=['hy'])

            NCS = SEGL // 16
            NC16 = NTOK // 16
            KSTEPS = 5 if is_p else 2

            def zseg(buf):
                return buf[:, :, 0:NC16].rearrange("p a (s c) -> p a s c", c=NCS)

            def ustr(j, s_, plo=0, phi=128):
                return ub[plo:phi, j, 0:NTOK].rearrange("p (n s) -> p n s", s=16)[:, :, s_]

            def ssm_Z():
                for j in range(4):
                    dma('sp', wzb[:].rearrange("p (s x) -> p s x", s=16),
                        wz_d[l, :, :, j, :, :].rearrange("s p r c -> p s (r c)"), 'wzb_ld', reads=['wz_d%d' % l],
                        writes=['wzb'])
                    wz4 = wzb[:].rearrange("p (s r c) -> p s r c", r=2, c=128)
                    for q in range(4):
                        pr = 4 * j + q
                        m = next_psM()
                        for ri in range(2):
                            def fn(e, q=q, ri=ri, m=m, j=j):
                                ins = None
                                for s_ in range(16):
                                    ins = e.matmul(psM[m][:, ri * NC16:(ri + 1) * NC16], wz4[32 * q:32 * q + 32, s_, ri, :],
                                                   ustr(j, s_, 32 * q, 32 * q + 32), start=(s_ == 0), stop=(s_ == 15),
                                                   tile_position=(32 * q, 0))
                                return ins
                            P.op('pe', fn, reads=['wzb', 'ub'], writes=['psM%d' % m])
                        P.op('act', (lambda pr, m: lambda e: e.copy(out=Zr[:, pr, 0:NC16], in_=psM[m][:, 0:NC16]))(pr, m),
                             reads=['psM%d' % m], writes=['Zr'])
                        P.op('act', (lambda pr, m: lambda e: e.copy(out=Zi[:, pr, 0:NC16], in_=psM[m][:, NC16:2 * NC16]))(pr, m),
                             reads=['psM%d' % m], writes=['Zi'])

            def chunk_scan():
                def co(k, c, shape):
                    a_ = lam[:, :, 4 + k, c]
                    return a_.unsqueeze(2).unsqueeze(3).broadcast_to(shape)
                t1 = rstd[:, 0:512].rearrange("p (a c) -> p a c", c=32)
                t2 = ef[0][:, 0:512].rearrange("p (a c) -> p a c", c=32)
                X = {'Zr': Zr, 'Zi': Zi, 'Pr': tmpf[0][:, 0:512].rearrange("p (a c) -> p a c", c=32),
                     'Pi': tmpf[1][:, 0:512].rearrange("p (a c) -> p a c", c=32)}
                KEY = {'Zr': 'Zr', 'Zi': 'Zi', 'Pr': 'tmpf0', 'Pi': 'tmpf1'}

                def TT(out_, in0_, in1_, op_, r, w):
                    P.op('dve', lambda e: e.tensor_tensor(out=out_, in0=in0_, in1=in1_, op=op_), reads=r, writes=w)
                sh1 = [128, 16, NSEG, 1]
                fr = zseg(Zr)[:, :, :, 0:1]; fi = zseg(Zi)[:, :, :, 0:1]
                cr = stR[:, l, :, 0:NSEG].unsqueeze(3); ci = stI[:, l, :, 0:NSEG].unsqueeze(3)
                u1 = zseg(t1)[:, :, :, 0:1]
                TT(u1, cr, co(0, 0, sh1), ALU.mult, ['stR', 'lam'], ['rstd'])
                TT(fr, fr, u1, ALU.add, ['rstd', 'Zr'], ['Zr'])
                TT(u1, ci, co(0, 1, sh1), ALU.mult, ['stI', 'lam', 'Zr'], ['rstd'])
                TT(fr, fr, u1, ALU.subtract, ['rstd', 'Zr'], ['Zr'])
                TT(u1, ci, co(0, 0, sh1), ALU.mult, ['stI', 'lam', 'Zr'], ['rstd'])
                TT(fi, fi, u1, ALU.add, ['rstd', 'Zi'], ['Zi'])
                TT(u1, cr, co(0, 1, sh1), ALU.mult, ['stR', 'lam', 'Zi'], ['rstd'])
                TT(fi, fi, u1, ALU.add, ['rstd', 'Zi'], ['Zi'])
                ar_, ai_, br_, bi_ = 'Zr', 'Zi', 'Pr', 'Pi'
                for k in range(KSTEPS):
                    s_ = 1 << k
                    w_ = NCS - s_
                    shp = [128, 16, NSEG, w_]
                    A_r = zseg(X[ar_]); A_i = zseg(X[ai_]); B_r = zseg(X[br_]); B_i = zseg(X[bi_])
                    v1 = zseg(t1)[:, :, :, 0:w_]; v2 = zseg(t2)[:, :, :, 0:w_]
                    TT(v1, A_r[:, :, :, 0:w_], co(k, 0, shp), ALU.mult, [KEY[ar_], 'lam'], ['rstd'])
                    TT(v2, A_i[:, :, :, 0:w_], co(k, 1, shp), ALU.mult, [KEY[ai_], 'lam'], ['ef0'])
                    TT(B_r[:, :, :, s_:NCS], A_r[:, :, :, s_:NCS], v1, ALU.add, [KEY[ar_], 'rstd'], [KEY[br_]])
                    TT(B_r[:, :, :, s_:NCS], B_r[:, :, :, s_:NCS], v2, ALU.subtract, [KEY[br_], 'ef0'], [KEY[br_]])
                    TT(B_r[:, :, :, 0:s_], A_r[:, :, :, 0:s_], A_r[:, :, :, 0:s_], ALU.max, [KEY[ar_]], [KEY[br_]])
                    TT(v1, A_i[:, :, :, 0:w_], co(k, 0, shp), ALU.mult, [KEY[ai_], 'lam', KEY[br_]], ['rstd'])
                    TT(v2, A_r[:, :, :, 0:w_], co(k, 1, shp), ALU.mult, [KEY[ar_], 'lam', KEY[br_]], ['ef0'])
                    TT(B_i[:, :, :, s_:NCS], A_i[:, :, :, s_:NCS], v1, ALU.add, [KEY[ai_], 'rstd'], [KEY[bi_]])
                    TT(B_i[:, :, :, s_:NCS], B_i[:, :, :, s_:NCS], v2, ALU.add, [KEY[bi_], 'ef0'], [KEY[bi_]])
                    TT(B_i[:, :, :, 0:s_], A_i[:, :, :, 0:s_], A_i[:, :, :, 0:s_], ALU.max, [KEY[ai_]], [KEY[bi_]])
                    ar_, br_ = br_, ar_
                    ai_, bi_ = bi_, ai_
                S_r = zseg(X[ar_]); S_i = zseg(X[ai_])
                for ri, S_, cc, kk in ((0, S_r, cr, KEY[ar_]), (1, S_i, ci, KEY[ai_])):
                    spv = Sp[:, ri, :, 0:NC16].rearrange("p a (s c) -> p a s c", c=NCS)
                    P.op('dve', (lambda spv, cc: lambda e: e.tensor_copy(out=spv[:, :, :, 0:1], in_=cc))(spv, cc),
                         reads=['stR', 'stI'], writes=['Sp'])
                    P.op('dve', (lambda spv, S_: lambda e: e.tensor_copy(out=spv[:, :, :, 1:NCS], in_=S_[:, :, :, 0:NCS - 1]))(spv, S_),
                         reads=[kk], writes=['Sp'])
                P.op('dve', lambda e: e.tensor_copy(out=cr, in_=S_r[:, :, :, NCS - 1:NCS]), reads=[KEY[ar_], 'Sp'], writes=['stR'])
                P.op('dve', lambda e: e.tensor_copy(out=ci, in_=S_i[:, :, :, NCS - 1:NCS]), reads=[KEY[ai_], 'Sp'], writes=['stI'])

            def ssm_Y_gen():
                for j in range(4):
                    dma('sp', bdb[:], bd_d[l, :, :, j, :].rearrange("t p c -> p t c"), 'bdb_ld', reads=['bd_d%d' % l],
                        writes=['bdb'])
                    dma('sp', wzb[:].rearrange("p (t x) -> p t x", t=16),
                        wy_d[l, :, :, 4 * j:4 * j + 4, :, :].rearrange("t p q r c -> p t (q r c)"), 'wzb_ld',
                        reads=['wy_d%d' % l], writes=['wzb'])
                    wy5 = wzb[:].rearrange("p (t q r c) -> p t q r c", q=4, r=2, c=32)
                    for t_ in range(16):
                        def fn(e, j=j, t_=t_, wy5=wy5):
                            ins = None
                            o_ = psT[:, 0:NTOK].rearrange("p (n s) -> p n s", s=16)[:, :, t_]
                            for s_ in range(t_ + 1):
                                ins = e.matmul(o_, bdb[:, t_ - s_, :], ustr(j, s_), start=(s_ == 0), stop=False)
                            for q in range(4):
                                oq = psT[32 * q:32 * q + 32, 0:NTOK].rearrange("p (n s) -> p n s", s=16)[:, :, t_]
                                for ri in range(2):
                                    ins = e.matmul(oq, wy5[:, t_, q, ri, :], Sp[:, ri, 4 * j + q, 0:NC16], start=False,
                                                   stop=(ri == 1), tile_position=(0, 32 * q))
                            return ins
                        P.op('pe', fn, reads=['bdb', 'wzb', 'ub', 'Sp'], writes=['psT'])
                        if t_ < 15:
                            yield
                    ti = rot('tmpf', 2)
                    P.op('dve', (lambda j, ti: lambda e: e.scalar_tensor_tensor(
                        out=tmpf[ti][:, 0:NTOK], in0=ub[:, j, 0:NTOK], scalar=dskT[:, l, j:j + 1], in1=psT[:, 0:NTOK],
                        op0=ALU.mult, op1=ALU.add))(j, ti),
                        reads=['ub', 'psT'], writes=['tmpf%d' % ti])
                    P.op('act', (lambda j, ti: lambda e: e.activation(out=gT[:, j, 0:NTOK], in_=tmpf[ti][:, 0:NTOK],
                                                                      func=AF.Gelu))(j, ti),
                         reads=['tmpf%d' % ti], writes=['gT'])
                    yield

            def ssm_glu():
                for c4 in range(4):
                    b = w_next()
                    ma = next_psM()
                    mm_group(psM[ma][:, 0:NTOK], [(wblk[b][:, kt, :], gT[:, kt, 0:NTOK]) for kt in range(4)],
                             reads=['wblk%d' % b, 'gT'], writes=['psM%d' % ma])
                    P.op('act', (lambda c4, ma: lambda e: e.activation(out=tmpf[0][:, 0:NTOK], in_=psM[ma][:, 0:NTOK],
                                                                       func=AF.Identity, bias=bgluT[:, l, c4:c4 + 1],
                                                                       scale=1.0))(c4, ma),
                         reads=['psM%d' % ma], writes=['tmpf0'])
                    ms = next_psM()
                    mm_group(psM[ms][:, 0:NTOK],
                             [(wblk[b][:, 4 + kt, :], gT[:, kt, 0:NTOK]) for kt in range(4)],
                             reads=['wblk%d' % b, 'gT'], writes=['psM%d' % ms])
                    P.op('act', (lambda c4, ms: lambda e: e.activation(out=tmpf[1][:, 0:NTOK], in_=psM[ms][:, 0:NTOK],
                                                                       func=AF.Sigmoid, bias=bgluT[:, l, 4 + c4:5 + c4],
                                                                       scale=1.0))(c4, ms),
                         reads=['psM%d' % ms], writes=['tmpf1'])
                    P.op('dve', (lambda c4: lambda e: e.tensor_tensor(out=hy[:, 8 + c4, 0:NTOK], in0=tmpf[0][:, 0:NTOK],
                                                                      in1=tmpf[1][:, 0:NTOK], op=ALU.mult))(c4),
                         reads=['tmpf0', 'tmpf1'], writes=['hy'])

            def pool_branch():
                EW = 15 + SEGL

                def up3(g):
                    return upT[:, g, 0:NSEG * EW].rearrange("p (s w) -> p s w", w=EW)

                def pt3(bufap):
                    return bufap[:, 0:NSEG * EW].rearrange("p (s w) -> p s w", w=EW)
                for g, w in enumerate((2, 4, 8, 16)):
                    src3 = up3(g)
                    srck = 'upT'
                    nst = g + 1
                    tbufs = [(pta, 'pta'), (ptb, 'ptb')]
                    for si_ in range(nst):
                        sh = 1 << si_
                        lo = 15 - (w - (sh << 1))
                        dst, dstk = tbufs[si_ % 2]
                        P.op('pool', (lambda dst, src3, lo, sh: lambda e: e.tensor_tensor(
                            out=pt3(dst)[:, :, lo:EW], in0=src3[:, :, lo:EW], in1=src3[:, :, lo - sh:EW - sh],
                            op=ALU.add))(dst, src3, lo, sh),
                            reads=[srck], writes=[dstk])
                        src3, srck = pt3(dst), dstk
                    P.op('dve', (lambda src3, g, w: lambda e: e.scalar_tensor_tensor(
                        out=tok3(sqb[0][:, 0:NTOK]), in0=src3[:, :, 15:EW], scalar=1.0 / w, in1=up3(g)[:, :, 15:EW],
                        op0=ALU.mult, op1=ALU.subtract))(src3, g, w),
                        reads=[srck, 'upT'], writes=['sqb0'])
                    if is_p and t == 0:
                        P.op('dve', (lambda src3, g: lambda e: e.tensor_tensor(
                            out=tmpf[0][:, 0:16], in0=src3[:, 0, 15:31], in1=invcnt[:, g, :], op=ALU.mult))(src3, g),
                            reads=[srck, 'invcnt'], writes=['tmpf0'])
                        P.op('dve', (lambda g: lambda e: e.tensor_tensor(
                            out=sqb[0][:, 0:16], in0=tmpf[0][:, 0:16], in1=upT[:, g, 15:31], op=ALU.subtract))(g),
                            reads=['tmpf0', 'upT'], writes=['sqb0'])
                    mpl = next_psM()
                    P.op('pe', (lambda g, mpl: lambda e: e.matmul(psM[mpl][:, 0:NTOK], wpl[:, g, :], sqb[0][:, 0:NTOK],
                                                                  start=True, stop=True))(g, mpl),
                         reads=['wpl', 'sqb0'], writes=['psM%d' % mpl])
                    P.op('act', (lambda g, mpl: lambda e: e.activation(out=hy[:, 12 + g, 0:NTOK], in_=psM[mpl][:, 0:NTOK],
                                                                       func=AF.Identity, scale=pscT[:, l, g:g + 1]))(g, mpl),
                         reads=['psM%d' % mpl], writes=['hy'])
                if need_out:
                    for sg in range(NSEG):
                        def fn(e, sg=sg):
                            ins = None
                            for g in range(4):
                                ins = e.transpose(psT[0:15, g * 128:(g + 1) * 128], up3(g)[:, sg, EW - 15:EW], identf[:, :])
                            return ins
                        P.op('pe', fn, reads=['upT', 'identf'], writes=['psT'])
                        P.op('dve', lambda e: e.tensor_copy(out=tmpf[1][0:15, :], in_=psT[0:15, :]), reads=['psT'],
                             writes=['tmpf1'])
                        dst = npp[l] if is_p else nps[l, sg]
                        dma('sp', dst, tmpf[1][0:15, :], 'pst_o', reads=['tmpf1'], writes=['o_pool'], out=True)
                if is_p and not last_p:
                    P.op('pool', lambda e: e.tensor_copy(out=uph[:, l, :, :], in_=upT[:, :, EW - 15:EW]),
                         reads=['upT'], writes=['uph'])

            def load_ebt(par):
                dma('sp', ebt[:].rearrange("p a c -> p (a c)"), eb_d[l, par], 'ebt_ld',
                    reads=['eb_d%d_%d' % (l, par)], writes=['ebt'])

            units = []
            if is_p:
                for par in range(2):
                    units.append(('ebt', par))
                    for j in range(8):
                        for mc in range(par, NCH, 2):
                            units.append(('att', j, mc, mc * 64))
            else:
                units.append(('ebt', 0))
                for sg in range(4):
                    units.append(('hist', sg))
                    for j in range(8):
                        units.append(('att', j, sg, sg * 64))
            ssm_Z()
            chunk_scan()
            ygen = ssm_Y_gen()
            n_att = sum(1 for u in units if u[0] == 'att')
            per = -(-64 // n_att)
            for u in units:
                if u[0] == 'ebt':
                    load_ebt(u[1])
                elif u[0] == 'hist':
                    load_hist(u[1])
                else:
                    attn_unit(u[1], u[2], u[3])
                    for _ in range(per):
                        next(ygen, None)
            for _ in ygen:
                pass
            _chk('m4')
            pool_branch()
            ssm_glu()
            _chk('m5')

            if DEBUG[0] and is_p and t == 0 and l == 0:
                dbg = nc.dram_tensor("dbg_hy", [128, KT * TT], BF16, kind="ExternalOutput").ap()
                dma('sp', dbg, hy[:].rearrange("p k c -> p (k c)"), 'dbg_o', reads=['hy'], writes=['o_dbg'], out=True)
                dbg3 = nc.dram_tensor("dbg_t0", [128, TT], F32, kind="ExternalOutput").ap()
                dma('sp', dbg3, tmpf[0][:], 'dbg_o', reads=['tmpf0'], writes=['o_dbg'], out=True)
                dbg4 = nc.dram_tensor("dbg_t1", [128, TT], F32, kind="ExternalOutput").ap()
                dma('sp', dbg4, tmpf[1][:], 'dbg_o', reads=['tmpf1'], writes=['o_dbg'], out=True)
                dbg2 = nc.dram_tensor("dbg_g", [128, 4 * TT], BF16, kind="ExternalOutput").ap()
                dma('sp', dbg2, gT[:].rearrange("p k c -> p (k c)"), 'dbg_o', reads=['gT'], writes=['o_dbg'], out=True)

            if need_out:
                for sg in range(NSEG):
                    for e_ in range(2):
                        dr = (nsrp[l] if is_p else nsrs[l, sg])
                        di = (nsip[l] if is_p else nsis[l, sg])
                        dma('sp', bass.AP(dr.tensor, dr.offset + e_ * 64, [[1, 64], [128, 16]]),
                            stR[64 * e_:64 * e_ + 64, l, :, sg], 'st_o', reads=['stR'], writes=['o_st'], out=True, nonc=True)
                        dma('sp', bass.AP(di.tensor, di.offset + e_ * 64, [[1, 64], [128, 16]]),
                            stI[64 * e_:64 * e_ + 64, l, :, sg], 'st_o', reads=['stI'], writes=['o_st'], out=True, nonc=True)

            for (f0, f1, width) in ((0, 8, 1024), (8, 12, 512), (12, 16, 512)):
                m = next_psM()
                for f in range(f0, f1):
                    qi = rot('sqb', 2)
                    P.op('act', (lambda f, qi: lambda e: e.activation(out=sqb[qi][:, 0:NTOK], in_=hy[:, f, 0:NTOK],
                                                                      func=AF.Square))(f, qi),
                         reads=['hy'], writes=['sqb%d' % qi])
                    P.op('pe', (lambda f, qi, m, f0, f1: lambda e: e.matmul(psM[m][:, 0:NTOK], onesb[:, :], sqb[qi][:, 0:NTOK],
                                                                            start=(f == f0), stop=(f == f1 - 1)))(f, qi, m, f0, f1),
                         reads=['sqb%d' % qi, 'onesb'], writes=['psM%d' % m])
                make_rstd(m, width)
                for f in range(f0, f1):
                    ti = rot('tmpf', 2)
                    P.op('dve', (lambda f, ti: lambda e: e.scalar_tensor_tensor(
                        out=tmpf[ti][:, 0:NTOK], in0=hy[:, f, 0:NTOK], scalar=bngT[:, l, f:f + 1], in1=rstd[:, 0:NTOK],
                        op0=ALU.mult, op1=ALU.mult))(f, ti),
                        reads=['hy', 'rstd'], writes=['tmpf%d' % ti])
                    P.op('dve', (lambda f, ti: lambda e: e.tensor_tensor(
                        out=hy[:, f, 0:NTOK], in0=tmpf[ti][:, 0:NTOK], in1=zT[:, f, 0:NTOK], op=ALU.mult))(f, ti),
                        reads=['tmpf%d' % ti, 'zT'], writes=['hy'])

            _chk('m6')
            for cb in range(16):
                b = w_next()
                m = next_psM()
                mm_group(psM[m][:, 0:NTOK], [(wblk[b][:, kt, :], hy[:, kt, 0:NTOK]) for kt in range(KT)],
                         reads=['wblk%d' % b, 'hy'], writes=['psM%d' % m])
                for sg in range(NSEG):
                    r = rows[sg]
                    c0, c1 = sg * SEGL, (sg + 1) * SEGL
                    P.op('dve', (lambda cb, m, r, c0, c1: lambda e: e.scalar_tensor_tensor(
                        out=xT[:, cb, c0:c1], in0=psM[m][:, c0:c1], scalar=modT[:, l, 32 + cb, r:r + 1],
                        in1=xT[:, cb, c0:c1], op0=ALU.mult, op1=ALU.add))(cb, m, r, c0, c1),
                        reads=['psM%d' % m, 'modT', 'xT'], writes=['xT'])

        for l_ in range(NLR):
            do_layer(l_)
        _chk('m7')

        m = next_psM()
        for kt in range(KT):
            qi = rot('sqb', 2)
            P.op('act', (lambda kt, qi: lambda e: e.activation(out=sqb[qi][:, 0:NTOK], in_=xT[:, kt, 0:NTOK],
                                                               func=AF.Square))(kt, qi),
                 reads=['xT'], writes=['sqb%d' % qi])
            P.op('pe', (lambda kt, qi, m: lambda e: e.matmul(psM[m][:, 0:NTOK], onesb[:, :], sqb[qi][:, 0:NTOK],
                                                             start=(kt == 0), stop=(kt == KT - 1)))(kt, qi, m),
                 reads=['sqb%d' % qi, 'onesb'], writes=['psM%d' % m])
        make_rstd(m, D)
        for kt in range(KT):
            P.op('dve', (lambda kt: lambda e: e.scalar_tensor_tensor(
                out=xT[:, kt, 0:NTOK], in0=xT[:, kt, 0:NTOK], scalar=fngT[:, kt:kt + 1], in1=rstd[:, 0:NTOK],
                op0=ALU.mult, op1=ALU.mult))(kt),
                reads=['xT', 'rstd'], writes=['xT'])
        ydst = yp[t * TT:(t + 1) * TT, :] if is_p else ys
        for tb in range(NB):
            for k4 in range(4):
                def fn(e, tb=tb, k4=k4):
                    ins = None
                    for kk in range(4):
                        kt = k4 * 4 + kk
                        ins = e.transpose(psT[:, kk * 128:(kk + 1) * 128], xT[:, kt, tb * 128:(tb + 1) * 128], identf[:, :])
                    return ins
                P.op('pe', fn, reads=['xT', 'identf'], writes=['psT'])
                P.op('act', (lambda k4: lambda e: e.copy(out=stg[:, k4 * 512:(k4 + 1) * 512], in_=psT[:, :]))(k4),
                     reads=['psT'], writes=['stg'])
            dma('sp', ydst[tb * 128:(tb + 1) * 128, :], stg[:], 'stg_o', reads=['stg'], writes=['o_y'], out=True)

    try:
        for (kind_, t_) in tiles:
            do_tile(kind_, t_)
    except _Stop:
        pass
    P.finish()
    P.emit()


def _consts():
    identf = np.eye(128, dtype=np.float32)
    j64 = np.ascontiguousarray(np.eye(64, dtype=np.float32)[::-1])
    invcnt = np.zeros((128, 4, 16), np.float32)
    for g, w in enumerate((2, 4, 8, 16)):
        for tt in range(16):
            invcnt[:, g, tt] = 1.0 / min(w, tt + 1)
    masks = np.zeros((128, 2), np.float32)
    masks[0:64, 0] = 1.0
    masks[64:128, 1] = 1.0
    bmask = np.kron(np.eye(4, dtype=np.float32), np.ones((32, 32), np.float32))
    return identf, j64, invcnt, masks, bmask


_NC_CACHE = {}


def _in_maps(inp, NPT, n_cores):
    identf, j64, invcnt, masks, bmask = _consts()
    f = lambda a: np.ascontiguousarray(np.asarray(a, dtype=np.float32))
    rel = f(inp['rel_bias'])
    idx = np.minimum(831 - np.arange(832), 512)
    xrb = np.ascontiguousarray(rel[:, :, idx])
    shared = dict(
        norm_g=f(inp['norm_g']), w_ada=f(inp['w_ada']), b_ada=f(inp['b_ada']), w_in=f(inp['w_in']), xrb=xrb,
        a_re=f(inp['ssm_a_re']), a_im=f(inp['ssm_a_im']), log_dt=f(inp['ssm_log_dt']),
        b_re=f(inp['ssm_b_re']), b_im=f(inp['ssm_b_im']), c_re=f(inp['ssm_c_re']), c_im=f(inp['ssm_c_im']),
        ssm_d=f(inp['ssm_d']).reshape(NL, 512), w_glu=f(inp['w_glu']), b_glu=f(inp['b_glu']),
        w_pool=f(inp['w_pool']), pool_scale=f(inp['pool_scale']), bng=f(inp['branch_norm_g']),
        w_out=f(inp['w_out']), fng=f(inp['final_norm_g']),
        identf=identf, j64=j64, invcnt=invcnt, masks=masks, bmask=bmask)
    maps = []
    for c in range(n_cores):
        sl = slice(4 * c, 4 * c + 4)
        m = dict(shared)
        m['xp'] = f(inp['x_prompt'][c, :NPT * TT])
        m['xs'] = f(inp['x_sample'][sl]).reshape(256, D)
        m['cvec'] = np.ascontiguousarray(np.concatenate([f(inp['c_prompt'][c:c + 1]), f(inp['c_sample'][sl])], 0))
        m['ck'] = f(inp['cache_k'][:, sl]).reshape(NL, 4, 512, 1024)
        m['cv'] = f(inp['cache_v'][:, sl]).reshape(NL, 4, 512, 1024)
        m['sre'] = f(inp['state_ssm_re'][:, sl])
        m['sim'] = f(inp['state_ssm_im'][:, sl])
        m['spool'] = f(inp['state_pool'][:, sl])
        maps.append(m)
    return maps


def run(inp, NPT=NPT_FULL, with_sample=True, nlayers=NL, n_cores=8):
    key = (NPT, with_sample, nlayers)
    if key not in _NC_CACHE:
        _NC_CACHE[key] = build_nc(NPT, with_sample, nlayers)
    nc = _NC_CACHE[key]
    maps = _in_maps(inp, NPT, n_cores)
    res = run_bass_kernel_spmd(nc, maps, core_ids=list(range(n_cores)))
    return res.results


def kernel(**inp):
    R = run(inp)
    st = lambda k: np.stack([np.asarray(r[k]) for r in R], 0)
    y_prompt = st('yp')
    y_sample = st('ys').reshape(32, 64, D)
    nkp = st('nkp').transpose(1, 0, 2, 3).reshape(NL, 8, 512, 16, 64)
    nvp = st('nvp').transpose(1, 0, 2, 3).reshape(NL, 8, 512, 16, 64)
    nsrp = st('nsrp').transpose(1, 0, 2, 3)
    nsip = st('nsip').transpose(1, 0, 2, 3)
    npp = st('npp').transpose(1, 0, 2, 3)
    nks = st('nks').reshape(8, NL, 4, 64, 16, 64).transpose(1, 0, 2, 3, 4, 5).reshape(NL, 32, 64, 16, 64)
    nvs = st('nvs').reshape(8, NL, 4, 64, 16, 64).transpose(1, 0, 2, 3, 4, 5).reshape(NL, 32, 64, 16, 64)
    nsrs = st('nsrs').transpose(1, 0, 2, 3, 4).reshape(NL, 32, 32, 64)
    nsis = st('nsis').transpose(1, 0, 2, 3, 4).reshape(NL, 32, 32, 64)
    nps = st('nps').transpose(1, 0, 2, 3, 4).reshape(NL, 32, 15, 512)
    outs = (y_prompt, y_sample, nkp, nvp, nsrp, nsip, npp, nks, nvs, nsrs, nsis, nps)
    return tuple(np.ascontiguousarray(o.astype(np.float32)) for o in outs)
```
